# Optimizing a Trainium2 kernel written in Bass

```python
import math
import jax, jax.numpy as jnp
from jax import lax
import numpy as np

D_MODEL = 1024
BATCH = 8
SEQ = 4096
DEPTH = 2

GRID_W = 64
CTX_LEN = 256

RET_HEADS = 4
RET_DK = 128
RET_DV = 256
RET_CHUNK = 128
RET_QK = RET_HEADS * RET_DK
RET_V = RET_HEADS * RET_DV

GLA_HEADS = 4
GLA_DK = 128
GLA_DV = 256
GLA_RANK = 16
GLA_CHUNK = 64
GLA_LOGIT_NORM = 16.0
GLA_QK = GLA_HEADS * GLA_DK
GLA_V = GLA_HEADS * GLA_DV

ROPE_BASE = 10000.0
EPS = 1e-6

IN_SPLITS = (RET_QK, RET_QK, RET_V, RET_V, GLA_QK, GLA_QK, GLA_V, GLA_V, GLA_RANK, D_MODEL, D_MODEL)
IN_WIDTH = 2 * RET_QK + 2 * RET_V + 2 * GLA_QK + 2 * GLA_V + GLA_RANK + 2 * D_MODEL

kernel_name = "hybrid_retention_gla_prefix_dit"


def rms_norm(x, gain):
    xf = x.astype(jnp.float32)
    y = xf * lax.rsqrt(jnp.mean(xf * xf, axis=-1, keepdims=True) + EPS)
    return (y * gain.astype(jnp.float32)).astype(x.dtype)


def modulation(cvec, w_ada, b_ada):
    m = jax.nn.silu(cvec) @ w_ada + b_ada
    return jnp.split(m, 3, axis=-1)


def split_columns(p):
    idx, acc = [], 0
    for w in IN_SPLITS[:-1]:
        acc += w
        idx.append(acc)
    return jnp.split(p, idx, axis=-1)


def to_heads(t, n_heads):
    b, l, w = t.shape
    return t.reshape(b, l, n_heads, w // n_heads).transpose(0, 2, 1, 3)


def from_heads(t):
    b, h, l, d = t.shape
    return t.transpose(0, 2, 1, 3).reshape(b, l, h * d)


def flip(t):
    return jnp.flip(t, axis=2)


def axial_rotary(rows):
    r_idx, c_idx = jnp.meshgrid(jnp.arange(rows), jnp.arange(GRID_W), indexing="ij")
    r_idx = r_idx.reshape(-1).astype(jnp.float32)
    c_idx = c_idx.reshape(-1).astype(jnp.float32)
    n_freq = RET_DK // 4
    inv_freq = ROPE_BASE ** (-jnp.arange(n_freq, dtype=jnp.float32) / n_freq)
    ang_r = r_idx[:, None] * inv_freq
    ang_c = c_idx[:, None] * inv_freq
    ang = jnp.stack([ang_r, ang_r, ang_c, ang_c], axis=1).reshape(-1, RET_DK)
    return jnp.cos(ang), jnp.sin(ang)


def apply_rotary(t, cos, sin):
    tr = t.reshape(*t.shape[:-1], 2, 2, RET_DK // 4)
    rot = jnp.concatenate([-tr[..., 1:, :], tr[..., :1, :]], axis=-2).reshape(t.shape)
    return t * cos + rot * sin


def retention_chunked(q, k, v, log_gamma, state0, strict):
    b, nh, seq, dk = q.shape
    dv = v.shape[-1]
    n = seq // RET_CHUNK
    qc = q.reshape(b, nh, n, RET_CHUNK, dk)
    kc = k.reshape(b, nh, n, RET_CHUNK, dk)
    vc = v.reshape(b, nh, n, RET_CHUNK, dv)
    pos = jnp.arange(RET_CHUNK, dtype=jnp.float32)
    diff = pos[:, None] - pos[None, :]
    keep = (diff > 0) if strict else (diff >= 0)
    decay = jnp.where(keep, jnp.exp(log_gamma[:, None, None] * jnp.where(keep, diff, 0.0)), 0.0)
    scores = jnp.einsum("bhnid,bhnjd->bhnij", qc, kc) * decay[None, :, None]
    intra = jnp.einsum("bhnij,bhnje->bhnie", scores, vc)
    lgv = log_gamma[None, :, None, None, None]
    k_dec = kc * jnp.exp(lgv * (RET_CHUNK - 1.0 - pos)[:, None])
    kv_chunk = jnp.einsum("bhncd,bhnce->bhnde", k_dec, vc)
    gamma_chunk = jnp.exp(log_gamma * RET_CHUNK)[None, :, None, None]

    def step(s, kv):
        return gamma_chunk * s + kv, s

    s_final, s_prev = lax.scan(step, state0, jnp.moveaxis(kv_chunk, 2, 0))
    s_prev = jnp.moveaxis(s_prev, 0, 2)
    q_dec = qc * jnp.exp(lgv * (pos + 1.0)[:, None])
    inter = jnp.einsum("bhncd,bhnde->bhnce", q_dec, s_prev)
    return (intra + inter).reshape(b, nh, seq, dv), s_final


def gla_chunked(q, k, v, log_a, state0, strict):
    b, nh, seq, dk = q.shape
    dv = v.shape[-1]
    n = seq // GLA_CHUNK
    qc = q.reshape(b, nh, n, GLA_CHUNK, dk)
    kc = k.reshape(b, nh, n, GLA_CHUNK, dk)
    vc = v.reshape(b, nh, n, GLA_CHUNK, dv)
    g = jnp.cumsum(log_a.reshape(b, nh, n, GLA_CHUNK, dk), axis=3)
    g_ref = g[:, :, :, GLA_CHUNK // 2 - 1:GLA_CHUNK // 2, :]
    g_last = g[:, :, :, -1:, :]
    q_rel = qc * jnp.exp(g - g_ref)
    k_rel = kc * jnp.exp(g_ref - g)
    pos = jnp.arange(GLA_CHUNK)
    keep = (pos[:, None] > pos[None, :]) if strict else (pos[:, None] >= pos[None, :])
    scores = jnp.where(keep, jnp.einsum("bhnid,bhnjd->bhnij", q_rel, k_rel), 0.0)
    intra = jnp.einsum("bhnij,bhnje->bhnie", scores, vc)
    kv_chunk = jnp.einsum("bhncd,bhnce->bhnde", kc * jnp.exp(g_last - g), vc)
    chunk_decay = jnp.exp(g_last[:, :, :, 0, :])

    def step(s, inp):
        a, kv = inp
        return a[..., None] * s + kv, s

    s_final, s_prev = lax.scan(step, state0, (jnp.moveaxis(chunk_decay, 2, 0), jnp.moveaxis(kv_chunk, 2, 0)))
    s_prev = jnp.moveaxis(s_prev, 0, 2)
    inter = jnp.einsum("bhncd,bhnde->bhnce", qc * jnp.exp(g), s_prev)
    return (intra + inter).reshape(b, nh, seq, dv), s_final


def gla_log_decay(z, w_up, b_up):
    logit = (z @ w_up + b_up).astype(jnp.float32)
    return to_heads(jax.nn.log_sigmoid(logit) / GLA_LOGIT_NORM, GLA_HEADS)


def token_mixers(u, w_in, ret_decay, gla_w_up, gla_b_up, rotary, init_states):
    p = u @ w_in
    rq, rk, rv, rg, gq, gk, gv, gg, glr, ma, mb = split_columns(p)
    s_rf0, s_rb0, s_gf0, s_gb0 = init_states
    rq = to_heads(rq, RET_HEADS) * RET_DK ** -0.5
    rk = to_heads(rk, RET_HEADS)
    if rotary is not None:
        cos, sin = rotary
        rq = apply_rotary(rq, cos, sin)
        rk = apply_rotary(rk, cos, sin)
    rv = to_heads(rv, RET_HEADS)
    log_gamma = jnp.log1p(-jnp.exp(ret_decay.astype(jnp.float32)))
    ret_f, s_rf = retention_chunked(rq, rk, rv, log_gamma[0], s_rf0, False)
    ret_b, s_rb = retention_chunked(flip(rq), flip(rk), flip(rv), log_gamma[1], s_rb0, True)
    ret = ret_f + flip(ret_b)
    gq = to_heads(gq, GLA_HEADS) * GLA_DK ** -0.5
    gk = to_heads(gk, GLA_HEADS)
    gv = to_heads(gv, GLA_HEADS)
    log_a_f = gla_log_decay(glr, gla_w_up[0], gla_b_up[0])
    log_a_b = gla_log_decay(glr, gla_w_up[1], gla_b_up[1])
    gla_f, s_gf = gla_chunked(gq, gk, gv, log_a_f, s_gf0, False)
    gla_b, s_gb = gla_chunked(flip(gq), flip(gk), flip(gv), flip(log_a_b), s_gb0, True)
    gla = gla_f + flip(gla_b)
    return (ret, gla, rg, gg, ma, mb), (s_rf, s_rb, s_gf, s_gb)


def head_group_norm(o, gain):
    of = o.astype(jnp.float32)
    mu = jnp.mean(of, axis=-1, keepdims=True)
    var = jnp.mean(jnp.square(of - mu), axis=-1, keepdims=True)
    return from_heads((of - mu) * lax.rsqrt(var + EPS)) * gain.astype(jnp.float32)


def head_rms_norm(o, gain):
    of = o.astype(jnp.float32)
    y = of * lax.rsqrt(jnp.mean(of * of, axis=-1, keepdims=True) + EPS)
    return from_heads(y) * gain.astype(jnp.float32)


def merge_branches(parts, ret_norm_gain, gla_norm_gain, w_branch_ret, w_branch_gla, w_out):
    ret, gla, rg, gg, ma, mb = parts
    dtype = rg.dtype
    y_ret = (head_group_norm(ret, ret_norm_gain).astype(dtype) * jax.nn.silu(rg)) @ w_branch_ret
    y_gla = (head_rms_norm(gla, gla_norm_gain).astype(dtype) * jax.nn.silu(gg)) @ w_branch_gla
    merged = jax.nn.sigmoid(ma) * y_ret + jax.nn.sigmoid(mb) * y_gla
    return merged @ w_out


def setup_inputs(seed: int = 0) -> dict:
    key = jax.random.key(seed)
    ks = jax.random.split(key, 18)

    def normal(k, shape, scale):
        return scale * jax.random.normal(k, shape, jnp.float32)

    ret_decay_base = -math.log(2.0) * (5.0 + jnp.arange(RET_HEADS, dtype=jnp.float32))
    return {
        "x": normal(ks[0], (BATCH, SEQ, D_MODEL), 1.0),
        "c": normal(ks[1], (BATCH, D_MODEL), 1.0),
        "ctx": normal(ks[2], (BATCH, CTX_LEN, D_MODEL), 1.0),
        "c_ctx": normal(ks[3], (D_MODEL,), 1.0),
        "norm_gain": 1.0 + normal(ks[4], (DEPTH, D_MODEL), 0.1),
        "w_ada": normal(ks[5], (DEPTH, D_MODEL, 3 * D_MODEL), 0.5 * D_MODEL ** -0.5),
        "b_ada": normal(ks[6], (DEPTH, 3 * D_MODEL), 0.02),
        "w_in": normal(ks[7], (DEPTH, D_MODEL, IN_WIDTH), D_MODEL ** -0.5),
        "ret_decay": ret_decay_base + normal(ks[8], (DEPTH, 2, RET_HEADS), 0.05),
        "gla_w_up": normal(ks[9], (DEPTH, 2, GLA_RANK, GLA_QK), GLA_RANK ** -0.5),
        "gla_b_up": normal(ks[10], (DEPTH, 2, GLA_QK), 0.1),
        "ret_norm_gain": 1.0 + normal(ks[11], (DEPTH, RET_V), 0.1),
        "gla_norm_gain": 1.0 + normal(ks[12], (DEPTH, GLA_V), 0.1),
        "w_branch_ret": normal(ks[13], (DEPTH, RET_V, D_MODEL), RET_V ** -0.5),
        "w_branch_gla": normal(ks[14], (DEPTH, GLA_V, D_MODEL), GLA_V ** -0.5),
        "w_out": normal(ks[15], (DEPTH, D_MODEL, D_MODEL), D_MODEL ** -0.5),
        "final_norm_gain": 1.0 + normal(ks[16], (D_MODEL,), 0.1),
    }


def reference(x, c, ctx, c_ctx, norm_gain, w_ada, b_ada, w_in, ret_decay, gla_w_up, gla_b_up,
              ret_norm_gain, gla_norm_gain, w_branch_ret, w_branch_gla, w_out, final_norm_gain):
    batch, n_latent = x.shape[0], x.shape[1]
    rows = n_latent // GRID_W
    rotary = axial_rotary(rows)
    zero_states = (
        jnp.zeros((batch, RET_HEADS, RET_DK, RET_DV), jnp.float32),
        jnp.zeros((batch, RET_HEADS, RET_DK, RET_DV), jnp.float32),
        jnp.zeros((batch, GLA_HEADS, GLA_DK, GLA_DV), jnp.float32),
        jnp.zeros((batch, GLA_HEADS, GLA_DK, GLA_DV), jnp.float32),
    )
    h_lat, h_ctx = x, ctx
    for l in range(DEPTH):
        shift_x, scale_x, gate_x = modulation(c, w_ada[l], b_ada[l])
        shift_c, scale_c, gate_c = modulation(c_ctx, w_ada[l], b_ada[l])
        u_ctx = rms_norm(h_ctx, norm_gain[l]) * (1.0 + scale_c) + shift_c
        ctx_parts, ctx_states = token_mixers(u_ctx, w_in[l], ret_decay[l], gla_w_up[l], gla_b_up[l],
                                             None, zero_states)
        u_lat = rms_norm(h_lat, norm_gain[l]) * (1.0 + scale_x[:, None, :]) + shift_x[:, None, :]
        lat_parts, _ = token_mixers(u_lat, w_in[l], ret_decay[l], gla_w_up[l], gla_b_up[l],
                                    rotary, ctx_states)
        h_lat = h_lat + gate_x[:, None, :] * merge_branches(
            lat_parts, ret_norm_gain[l], gla_norm_gain[l], w_branch_ret[l], w_branch_gla[l], w_out[l])
        if l < DEPTH - 1:
            h_ctx = h_ctx + gate_c * merge_branches(
                ctx_parts, ret_norm_gain[l], gla_norm_gain[l], w_branch_ret[l], w_branch_gla[l], w_out[l])
    return rms_norm(h_lat, final_norm_gain)
```

```python
import math
import numpy as np
import concourse.bass as bass
import concourse.mybir as mybir
from concourse.bass_utils import run_bass_kernel_spmd

F32 = mybir.dt.float32
BF16 = mybir.dt.bfloat16
AF = mybir.ActivationFunctionType
ALU = mybir.AluOpType

P = 128
D = 1024
KC = 8
NH = 4
DK = 128
DV = 256
EPS = 1e-6
QSCALE = DK ** -0.5
LNQ = math.log(QSCALE)
SAME_ENGINE_SYNC = True
import os
DBG_STOP = os.environ.get("KDBG", "")
DBG2 = int(os.environ.get("KDBG2", "0"))


class Res:
    __slots__ = ("w", "r")

    def __init__(self):
        self.w = None
        self.r = {}


class V:
    __slots__ = ("ap", "res")

    def __init__(self, ap, res):
        self.ap = ap
        self.res = tuple(res)

    def __getitem__(self, idx):
        return V(self.ap[idx], self.res)


def _resl(vs):
    out = []
    for v in vs:
        if isinstance(v, V):
            for r in v.res:
                if r not in out:
                    out.append(r)
    return out


class Sched:
    ENG = ("pe", "dve", "act", "pool", "sp")

    def __init__(self, nc, ndma=40):
        self.nc = nc
        self.e = {"pe": nc.tensor, "dve": nc.vector, "act": nc.scalar, "pool": nc.gpsimd, "sp": nc.sync}
        self.sem = {k: nc.alloc_semaphore("s_" + k) for k in self.ENG}
        self.cnt = {k: 0 for k in self.ENG}
        self.dsem = [nc.alloc_semaphore("d%d" % i) for i in range(ndma)]
        self.dcnt = [0] * ndma
        self.dnext = 0
        self.seen = {k: {} for k in self.ENG}
        self.ninstr = 0

    def _wait(self, eng, tok, raw=True):
        key, val = tok
        if key == eng and (eng == "pe" or not SAME_ENGINE_SYNC or not raw):
            return
        s = self.seen[eng]
        if s.get(key, 0) >= val:
            return
        s[key] = val
        sem = self.sem[key] if isinstance(key, str) else self.dsem[key]
        self.e[eng].wait_ge(sem, val)

    def deps(self, eng, reads, writes):
        for r in reads:
            if r.w is not None:
                self._wait(eng, r.w)
        for r in writes:
            if r.w is not None:
                self._wait(eng, r.w, raw=False)
            for k, v in r.r.items():
                self._wait(eng, (k, v), raw=False)

    def mark(self, tok, reads, writes):
        k, v = tok
        for r in reads:
            if r.r.get(k, 0) < v:
                r.r[k] = v
        for r in writes:
            r.w = tok
            r.r = {}

    def op(self, eng, ins, reads, writes):
        reads = _resl(reads)
        writes = _resl(writes)
        self.deps(eng, reads, writes)
        i = ins()
        self.cnt[eng] += 1
        i.then_inc(self.sem[eng], 1)
        self.mark((eng, self.cnt[eng]), reads, writes)
        self.ninstr += 1
        return i

    def dma(self, eng, out, in_, **kw):
        reads = _resl([in_])
        writes = _resl([out])
        k = self.dnext
        self.dnext = (self.dnext + 1) % len(self.dsem)
        if self.dcnt[k] > 0:
            self._wait(eng, (k, self.dcnt[k]))
        self.deps(eng, reads, writes)
        i = self.e[eng].dma_start(out=out.ap, in_=in_.ap, **kw)
        self.dcnt[k] += 16
        i.then_inc(self.dsem[k], 16)
        self.mark((k, self.dcnt[k]), reads, writes)
        self.ninstr += 1
        return i

    def barrier(self):
        for eng in self.ENG:
            for o in self.ENG:
                if o != eng and self.cnt[o] > 0:
                    self._wait(eng, (o, self.cnt[o]))
            for k in range(len(self.dsem)):
                if self.dcnt[k] > 0:
                    self._wait(eng, (k, self.dcnt[k]))

    def mm(self, out, pairs, extra_reads=()):
        nc = self.nc
        reads = []
        for a, b in pairs:
            reads += [a, b]
        reads = _resl(list(reads) + list(extra_reads))
        writes = _resl([out])
        self.deps("pe", reads, writes)
        n = len(pairs)
        for i, (a, b) in enumerate(pairs):
            ins = nc.tensor.matmul(out.ap, lhsT=a.ap, rhs=b.ap, start=(i == 0), stop=(i == n - 1))
            self.ninstr += 1
        self.cnt["pe"] += 1
        ins.then_inc(self.sem["pe"], 1)
        self.mark(("pe", self.cnt["pe"]), reads, writes)

    def transposes(self, items, ident):
        nc = self.nc
        reads = _resl([b for _, b in items] + [ident])
        writes = _resl([a for a, _ in items])
        self.deps("pe", reads, writes)
        for a, b in items:
            ins = nc.tensor.transpose(a.ap, b.ap, ident.ap)
            self.ninstr += 1
        self.cnt["pe"] += 1
        ins.then_inc(self.sem["pe"], 1)
        self.mark(("pe", self.cnt["pe"]), reads, writes)

    def act(self, out, in_, func, scale=1.0, bias=0.0, accum=None):
        nc = self.nc
        reads = [in_, scale, bias]
        writes = [out, accum]
        sc = scale.ap if isinstance(scale, V) else float(scale)
        bi = bias.ap if isinstance(bias, V) else float(bias)
        kw = {}
        if accum is not None:
            kw["accum_out"] = accum.ap
        return self.op("act", lambda: nc.scalar.activation(out=out.ap, in_=in_.ap, func=func, bias=bi, scale=sc, **kw),
                       reads, writes)

    def tt(self, eng, out, in0, in1, op):
        e = self.e[eng]
        return self.op(eng, lambda: e.tensor_tensor(out=out.ap, in0=in0.ap, in1=in1.ap, op=op), [in0, in1], [out])

    def ts(self, eng, out, in0, s1, op0, s2=None, op1=None):
        e = self.e[eng]
        a1 = s1.ap if isinstance(s1, V) else float(s1)
        a2 = None if s2 is None else (s2.ap if isinstance(s2, V) else float(s2))
        kw = {}
        if op1 is not None:
            kw["op1"] = op1
        return self.op(eng, lambda: e.tensor_scalar(out=out.ap, in0=in0.ap, scalar1=a1, scalar2=a2, op0=op0, **kw),
                       [in0, s1, s2], [out])

    def stt(self, out, in0, scalar, in1, op0, op1):
        nc = self.nc
        sc = scalar.ap if isinstance(scalar, V) else float(scalar)
        return self.op("dve", lambda: nc.vector.scalar_tensor_tensor(out=out.ap, in0=in0.ap, scalar=sc, in1=in1.ap,
                                                                     op0=op0, op1=op1), [in0, scalar, in1], [out])

    def copy(self, eng, out, in_):
        if eng == "act":
            return self.act(out, in_, AF.Copy)
        e = self.e[eng]
        return self.op(eng, lambda: e.tensor_copy(out=out.ap, in_=in_.ap), [in_], [out])

    def memset(self, eng, out, val):
        e = self.e[eng]
        return self.op(eng, lambda: e.memset(out.ap, val), [], [out])


class Arena:
    def __init__(self, nc, nbytes):
        self.words = nbytes // 4
        self.t = nc.alloc_sbuf_tensor("arena", [P, self.words], F32).ap()
        self.off = 0
        self.peak = 0

    def alloc(self, shape, dt=F32, res=None):
        parts = shape[0]
        n = 1
        for s in shape[1:]:
            n *= s
        esz = 4 if dt == F32 else 2
        nw = (n * esz + 3) // 4
        nw = (nw + 7) // 8 * 8
        assert self.off + nw <= self.words, "arena overflow: need %d have %d" % (self.off + nw, self.words)
        ap = self.t[0:parts, self.off:self.off + nw]
        self.off += nw
        self.peak = max(self.peak, self.off)
        if dt != F32:
            ap = ap.bitcast(dt)
        ap = ap[:, 0:n]
        if len(shape) == 3:
            ap = ap.rearrange("p (a b) -> p a b", a=shape[1])
        return V(ap, [res if res is not None else Res()])


def build_program(NCT, NL, L):
    T = NCT + NL
    NT = T * P
    nc = bass.Bass("TRN2", target_bir_lowering=False)
    S = Sched(nc)

    def din(name, shape):
        return V(nc.dram_tensor(name, list(shape), F32, kind="ExternalInput").ap(), [])

    x_d = din("x", [NL * P, D])
    ctx_d = din("ctx", [NCT * P, D])
    ccol_d = din("ccol", [P, 16])
    wada_d = din("wada", [L, P, KC, 3072])
    badac_d = din("badac", [L, P, 16])
    bgate_d = din("bgate", [L, P, D])
    ngain_d = din("ngain", [L, P, KC])
    rgain_d = din("rgain", [L, P, D])
    ggain_d = din("ggain", [L, P, D])
    fgain_d = din("fgain", [P, D])
    rdec_d = din("rdec", [L, P, 8])
    wup_d = din("wup", [L, 17, 1024])
    wret_d = din("wret", [L, NH, P, KC, 1024])
    wgla_d = din("wgla", [L, NH, P, KC, 768])
    wglr_d = din("wglr", [L, P, KC, 16])
    wm_d = din("wm", [L, 2, P, KC, 1024])
    wbr_d = din("wbr", [L, 2, P, KC, 1024])
    wout_d = din("wout", [L, P, KC, 1024])
    cst_d = din("cst", [14, P, P])
    cos_d = din("cosT", [P, NL * P])
    sin_d = din("sinT", [P, NL * P])
    y_d = V(nc.dram_tensor("y", [NL * P, D], F32, kind="ExternalOutput").ap(), [])
    y_r = [Res() for _ in range(NL)]

    def dscr(name, shape, dt):
        return nc.dram_tensor(name, list(shape), dt, kind="Internal").ap()

    hlat_s = [dscr("hlat%d" % l, [NL * P, D], F32) for l in range(L - 1)]
    hctx_s = [dscr("hctx%d" % l, [NCT * P, D], F32) for l in range(L - 1)]
    oret_s = [dscr("oret%d" % l, [NT, D], BF16) for l in range(L)]
    ogla_s = [dscr("ogla%d" % l, [NT, D], BF16) for l in range(L)]
    m1_s = [dscr("m1_%d" % l, [NT, D], F32) for l in range(L)]
    glr_s = [dscr("glr_%d" % l, [17, NT], BF16) for l in range(L)]
    hlat_r = [[Res() for _ in range(NL)] for _ in range(L - 1)]
    hctx_r = [[Res() for _ in range(NCT)] for _ in range(L - 1)]
    oret_r = [[Res() for _ in range(T)] for _ in range(L)]
    ogla_r = [[Res() for _ in range(T)] for _ in range(L)]
    m1_r = [[Res() for _ in range(T)] for _ in range(L)]

    def rows(ap, t, res):
        return V(ap[t * P:(t + 1) * P, :], [res])

    psum_all = nc.alloc_psum_tensor("psum_all", [P, 8 * 512], F32).ap()
    pb = [V(psum_all[:, i * 512:(i + 1) * 512], [Res()]) for i in range(8)]
    pstate = {"i": 0}

    def bank2():
        m_ = pstate.get("mod", 8)
        if pstate["i"] % 2 == 1:
            pstate["i"] = (pstate["i"] + 1) % m_
        if pstate["i"] >= m_:
            pstate["i"] = 0
        i0 = pstate["i"]
        pstate["i"] = (pstate["i"] + 2) % m_
        ap = psum_all[:, i0 * 512:(i0 + 2) * 512].rearrange("p (a b) -> p a b", a=2)
        return V(ap, [pb[i0].res[0], pb[i0 + 1].res[0]])

    def bank():
        m_ = pstate.get("mod", 8)
        if pstate["i"] >= m_:
            pstate["i"] = 0
        b = pb[pstate["i"]]
        pstate["i"] = (pstate["i"] + 1) % m_
        return b

    def bank_bf(b):
        return V(b.ap.bitcast(BF16), b.res)

    A = Arena(nc, 212800)
    uT = A.alloc([P, KC, NT], BF16)
    uT_r = [Res() for _ in range(T)]

    def uT_v(kc, t0, t1):
        return V(uT.ap[:, kc, t0 * P:t1 * P], uT_r[t0:t1])

    cst = A.alloc([P, 13, P], F32)
    ident_bf = A.alloc([P, P], BF16)
    ones_bf = A.alloc([P, P], BF16)
    C_ID, C_MF, C_MB, C_DPOS, C_DNEG, C_IPOS, C_INEG, C_TRIF, C_TRIB, C_REVF, C_REVB, C_ONES, C_COLC = range(13)

    def cs(i):
        return cst[:, i, :]

    silu_col = A.alloc([P, 16], F32)
    modc = A.alloc([P, 16, 2], F32)
    Amod = A.alloc([P, KC, 2], F32)
    small = A.alloc([P, 64], F32)
    LG = A.alloc([P, 8], F32)
    wcol = A.alloc([P, 8], F32)
    g128 = A.alloc([P, 8], F32)
    pers_mark = A.off

    S.dma("sp", cst, V(cst_d.ap[0:13].rearrange("c p f -> p c f"), []))
    S.dma("pool", ident_bf, cst_d[C_ID])
    S.dma("pool", ones_bf, cst_d[C_ONES])
    pm_bf = A.alloc([P, P], BF16)
    S.dma("pool", pm_bf, cst_d[13])
    S.dma("sp", silu_col, ccol_d)
    S.act(silu_col, silu_col, AF.Silu)

    def rstd_from(out, in_, scale, eng_tmp):
        S.act(eng_tmp, in_, AF.Ln, scale=scale, bias=eps_ap)
        S.act(out, eng_tmp, AF.Exp, scale=-0.5)

    eps_t = A.alloc([P, 1], F32)
    S.memset("dve", eps_t, EPS)
    one_t = A.alloc([P, 1], F32)
    S.memset("dve", one_t, 1.0)
    lnq_t = A.alloc([P, 1], F32)
    S.memset("dve", lnq_t, LNQ)
    eps_ap = eps_t
    pers_mark = A.off

    for l in range(L):
        last = (l == L - 1)
        ctx_out = not last
        out_tiles = list(range(T)) if ctx_out else list(range(NCT, T))
        A.off = pers_mark
        S.barrier()
        badac = A.alloc([P, 16], F32)
        ngain = A.alloc([P, KC], F32)
        S.dma("sp", badac, wrapd(badac_d, l))
        S.dma("sp", ngain, wrapd(ngain_d, l))
        wblk = [A.alloc([P, KC, 512], F32) for _ in range(2)]
        pmod = bank()
        for blk in range(4):
            wb_ = wblk[blk % 2]
            S.dma("sp", wb_, V(wada_d.ap[l, :, :, blk * 512:(blk + 1) * 512], []))
            for j in range(4):
                jj = blk * 4 + j
                S.mm(pmod[:, jj * 2:jj * 2 + 2],
                     [(wb_[:, kc, j * P:(j + 1) * P], silu_col[:, kc * 2:kc * 2 + 2]) for kc in range(KC)])
        pm3 = V(pmod.ap[:, 0:32].rearrange("p (a b) -> p a b", b=2), pmod.res)
        for s in range(2):
            S.tt("dve", modc[:, :, s], pm3[:, :, s], badac, ALU.add)
            S.stt(Amod[:, :, s], modc[:, 8:16, s], 1.0, ngain, ALU.add, ALU.mult)
        S.barrier()
        A.off = pers_mark
        hts = [A.alloc([P, D], F32) for _ in range(3)]
        xns = [A.alloc([P, D], BF16) for _ in range(2)]
        junk = A.alloc([P, D], BF16)
        sst = [A.alloc([P, 4], F32) for _ in range(2)]
        bst_t = [A.alloc([P, 16], F32) for _ in range(2)]
        def pb_A(k, t):
            if l == 0:
                src = V(ctx_d.ap[t * P:(t + 1) * P, :], []) if t < NCT else V(x_d.ap[(t - NCT) * P:(t - NCT + 1) * P, :], [])
            else:
                src = rows(hctx_s[l - 1], t, hctx_r[l - 1][t]) if t < NCT else rows(hlat_s[l - 1], t - NCT, hlat_r[l - 1][t - NCT])
            S.dma("sp", hts[k % 3], src)

        def pb_B(k, t):
            ht = hts[k % 3]
            xn = xns[k % 2]
            ss = sst[k % 2]
            bs_ = bst_t[k % 2]
            for hf_ in range(2):
                S.op("dve", lambda: nc.vector.bn_stats(out=bs_.ap[:, hf_ * 6:hf_ * 6 + 6], in_=ht.ap[:, hf_ * 512:(hf_ + 1) * 512]), [ht], [bs_])
            S.op("dve", lambda: nc.vector.bn_aggr(out=bs_.ap[:, 12:14], in_=bs_.ap[:, 0:12]), [bs_], [bs_])
            S.tt("dve", bs_[:, 14:15], bs_[:, 12:13], bs_[:, 12:13], ALU.mult)
            S.tt("dve", ss[:, 0:1], bs_[:, 14:15], bs_[:, 13:14], ALU.add)
            S.act(ss[:, 1:2], ss[:, 0:1], AF.Ln, scale=1.0, bias=eps_t)
            S.act(ss[:, 2:3], ss[:, 1:2], AF.Exp, scale=-0.5)
            S.act(xn, ht, AF.Copy, scale=ss[:, 2:3])

        def pb_C(k, t):
            s_ = 1 if t < NCT else 0
            xn = xns[k % 2]
            pt = bank_bf(bank())
            S.transposes([(pt[:, kc * P:(kc + 1) * P], xn[:, kc * P:(kc + 1) * P]) for kc in range(KC)], ident_bf)
            for kc in range(KC):
                o = V(uT.ap[:, kc, t * P:(t + 1) * P], [uT_r[t]])
                S.ts("dve", o, pt[:, kc * P:(kc + 1) * P], Amod[:, kc, s_:s_ + 1], ALU.mult, modc[:, kc, s_:s_ + 1], ALU.add)

        pipeline([pb_A, pb_B, pb_C], list(range(T)))

        groups = [(0, NCT)] if NCT > 0 else []
        g = NCT
        while g < T:
            groups.append((g, min(g + 4, T)))
            g += 4
        bwd_order = list(range(NCT - 1, -1, -1)) + list(range(T - 1, NCT - 1, -1))
        assert NCT % 2 == 0 and NL % 2 == 0
        bwd_pairs = list(range(NCT - 2, -1, -2)) + list(range(T - 2, NCT - 1, -2))

        for br in range(2):
            S.barrier()
            A.off = pers_mark
            gain_src = rgain_d if br == 0 else ggain_d
            if br == 0:
                rdec = A.alloc([P, 8], F32)
                S.dma("sp", rdec, wrapd(rdec_d, l))
                Dtot1 = [A.alloc([P, P], F32) for _ in range(NH)]
                Gf1 = [A.alloc([P, P], F32) for _ in range(NH)]
                Gb1 = [A.alloc([P, P], F32) for _ in range(NH)]
                DtP = A.alloc([P, 2 * P], F32)
                GfP = A.alloc([P, 2 * P], F32)
                GbP = A.alloc([P, 2 * P], F32)
                S.act(LG, rdec, AF.Exp)
                S.act(LG, LG, AF.Ln, scale=-1.0, bias=one_t)
                tmpa = A.alloc([P, P], F32)
                tmpb = A.alloc([P, P], F32)
                for h in range(NH):
                    S.act(tmpa, cs(C_DPOS), AF.Exp, scale=LG[:, h:h + 1], bias=lnq_t)
                    S.act(tmpb, cs(C_DNEG), AF.Exp, scale=LG[:, 4 + h:5 + h], bias=lnq_t)
                    S.tt("dve", tmpa, tmpa, cs(C_MF), ALU.mult)
                    S.tt("dve", tmpb, tmpb, cs(C_MB), ALU.mult)
                    S.tt("dve", Dtot1[h], tmpa, tmpb, ALU.add)
                    S.act(Gf1[h], cs(C_IPOS), AF.Exp, scale=LG[:, h:h + 1], bias=lnq_t)
                    S.act(Gb1[h], cs(C_INEG), AF.Exp, scale=LG[:, 4 + h:5 + h], bias=lnq_t)
                    S.act(wcol[:, h:h + 1], cst[:, C_COLC, 0:1], AF.Exp, scale=LG[:, h:h + 1])
                    S.act(wcol[:, 4 + h:5 + h], cst[:, C_COLC, 1:2], AF.Exp, scale=LG[:, 4 + h:5 + h])
                    S.act(g128[:, h:h + 1], cst[:, C_COLC, 2:3], AF.Exp, scale=LG[:, h:h + 1])
                    S.act(g128[:, 4 + h:5 + h], cst[:, C_COLC, 2:3], AF.Exp, scale=LG[:, 4 + h:5 + h])
            qT = A.alloc([P, NT], BF16)
            kT = A.alloc([P, NT], BF16)
            qT_r = [Res() for _ in range(T)]
            kT_r = [Res() for _ in range(T)]
            vv = A.alloc([P, T, DV], BF16)
            gs = A.alloc([P, T, DV], BF16)
            v_r = [Res() for _ in range(T)]
            gs_r = [Res() for _ in range(T)]
            sball = A.alloc([P, T, DV], BF16)
            sb_r = [Res() for _ in range(T)]
            ktok = A.alloc([P, T, DK], BF16)
            ktok_r = [Res() for _ in range(T)]
            if br == 0:
                NCOL = 1024
                wsrc = wret_d
            else:
                NCOL = 768
                wsrc = wgla_d
                glr_r = [Res() for _ in range(T)]
                glrg = [A.alloc([17, 512], BF16) for _ in range(1)] * 2
                glrt = [A.alloc([17, 2 * P], BF16) for _ in range(2)] * 2
                wup_hs = [A.alloc([17, 2 * P], BF16) for _ in range(2)]
                wglr = A.alloc([P, KC, 16], BF16)
                S.dma("pool", wglr, wrapd(wglr_d, l))
                S.memset("dve", glrg[0], 1.0)
                for gi_, (t0, t1) in enumerate(groups):
                    n = (t1 - t0) * P
                    pg = bank()
                    S.mm(pg[0:16, 0:n], [(wglr[:, kc, :], uT_v(kc, t0, t1)) for kc in range(KC)])
                    gg_ = glrg[gi_ % 2]
                    S.act(gg_[0:16, 0:n], pg[0:16, 0:n], AF.Copy)
                    S.dma("sp", V(glr_s[l][:, t0 * P:t1 * P], glr_r[t0:t1]), gg_[:, 0:n])
                glr_i = {"i": 0}

                def glr_load(t):
                    b_ = glrt[glr_i["i"] % 4]
                    glr_i["i"] += 1
                    S.dma("sp", b_, V(glr_s[l][:, t * P:(t + 2) * P], glr_r[t:t + 2]))
                    return b_
            wsb = A.alloc([P, KC, NCOL], BF16)
            Sf2 = [A.alloc([P, DV], F32) for _ in range(2)]
            Sb2 = [A.alloc([P, DV], F32) for _ in range(2)]
            Sfb2 = [A.alloc([P, DV], BF16) for _ in range(2)]
            NB = 3
            sgs = [A.alloc([P, 2, DV], F32) for _ in range(1)] * 2
            wk_kd = [A.alloc([P, 2 * P], BF16) for _ in range(NB)]
            wk_q = [[A.alloc([P, 2 * P], BF16) for _ in range(NB)] for _ in range(4 if br == 1 else 2)]
            wk_sm = [A.alloc([P, (4 if br == 1 else 2) * P], BF16) for _ in range(NB)]
            af_t = [A.alloc([P, 4], F32) for _ in range(NB)]
            if br == 0:
                costs = [A.alloc([P, 512], F32) for _ in range(1)] * 2
                sints = [A.alloc([P, 512], F32) for _ in range(1)] * 2
                rawb = A.alloc([P, 512], BF16)
                t1s = [A.alloc([P, 512], F32) for _ in range(1)]
                t2s = [A.alloc([P, 512], F32) for _ in range(1)]
            else:
                nls_t = [A.alloc([P, 4 * P], F32) for _ in range(2)] + [None]
                nls_t[2] = None
                ex_t = [A.alloc([P, 4 * P], F32) for _ in range(1)] * 2
                E1_t = [A.alloc([P, 4 * P], F32) for _ in range(1)] * 2
                E2_t = [A.alloc([P, 4 * P], F32) for _ in range(1)] * 2
                e3_t = [A.alloc([P, 2 * P], F32) for _ in range(1)] * 2
                mask4 = A.alloc([P, 4 * P], BF16)
                for i_ in range(4):
                    S.copy("dve", mask4[:, i_ * P:(i_ + 1) * P], cs(C_MF if i_ < 2 else C_MB))
            on_t = [A.alloc([P, 2, DV], F32) for _ in range(1)] * 2
            ot_t = [A.alloc([P, 2, DV], BF16) for _ in range(2)] + [None]
            ot_t[2] = ot_t[0]
            st_t = [A.alloc([P, 32], F32) for _ in range(3)]
            gh2s = [A.alloc([P, 2, DV], F32) for _ in range(1)] * 2

            def flat(v_):
                return V(v_.ap.rearrange("p a b -> p (a b)"), v_.res)

            def qT_v(t0, t1):
                return V(qT.ap[:, t0 * P:t1 * P], qT_r[t0:t1])

            def kT_v(t0, t1):
                return V(kT.ap[:, t0 * P:t1 * P], kT_r[t0:t1])

            def tl(arr, rl, t):
                return V(arr.ap[:, t, :], [rl[t]])

            def load_head_w(hh):
                for kc in range(KC):
                    S.dma("pool", wsb[:, kc, :], V(wsrc.ap[l, hh, :, kc, :], []))
                for n_ in range(2):
                    S.dma("sp", gh2s[hh % 2][:, n_, :], V(gain_src.ap[l, :, hh * DV:(hh + 1) * DV], []))
                if br == 1:
                    S.dma("pool", wup_hs[hh % 2], V(wup_d.ap[l, :, hh * 2 * P:(hh * 2 + 2) * P], []))

            load_head_w(0)
            if DBG_STOP == "pre%d" % br:
                S.barrier()
                return nc
            for h in range(NH):
                gh2 = gh2s[h % 2]
                if br == 1:
                    wup_h = wup_hs[h % 2]
                if br == 0:
                    for n_ in range(2):
                        S.copy("dve", DtP[:, n_ * P:(n_ + 1) * P], Dtot1[h])
                        S.copy("dve", GfP[:, n_ * P:(n_ + 1) * P], Gf1[h])
                        S.copy("dve", GbP[:, n_ * P:(n_ + 1) * P], Gb1[h])

                def proj_fm(gi, grp):
                    t0, t1 = grp
                    n = (t1 - t0) * P
                    is_ctx = t0 < NCT
                    if br == 0 and not is_ctx:
                        ct = costs[gi % 2]
                        st = sints[gi % 2]
                        lt0 = (t0 - NCT) * P
                        S.dma("sp", ct[:, 0:n], V(cos_d.ap[:, lt0:lt0 + n], []))
                        S.dma("sp", st[:, 0:n], V(sin_d.ap[:, lt0:lt0 + n], []))
                        for (blk, dstv) in ((0, qT_v(t0, t1)), (2, kT_v(t0, t1))):
                            pa = bank()
                            pbk = bank()
                            S.mm(pa[:, 0:n], [(wsb[:, kc, blk * P:(blk + 1) * P], uT_v(kc, t0, t1)) for kc in range(KC)])
                            S.copy("dve", rawb[:, 0:n], pa[:, 0:n])
                            S.mm(pbk[:, 0:n], [(pm_bf, rawb[:, 0:n])])
                            ta = t1s[0]
                            tb = t2s[0]
                            S.tt("dve", ta[:, 0:n], pa[:, 0:n], ct[:, 0:n], ALU.mult)
                            S.tt("dve", tb[:, 0:n], pbk[:, 0:n], st[:, 0:n], ALU.mult)
                            S.tt("pool", dstv, ta[:, 0:n], tb[:, 0:n], ALU.add)
                    else:
                        qb, kb = (0, 2) if br == 0 else (0, 1)
                        pa = bank()
                        S.mm(pa[:, 0:n], [(wsb[:, kc, qb * P:(qb + 1) * P], uT_v(kc, t0, t1)) for kc in range(KC)])
                        S.act(qT_v(t0, t1), pa[:, 0:n], AF.Copy, scale=(1.0 if br == 0 else QSCALE))
                        pbk = bank()
                        S.mm(pbk[:, 0:n], [(wsb[:, kc, kb * P:(kb + 1) * P], uT_v(kc, t0, t1)) for kc in range(KC)])
                        S.copy("dve", kT_v(t0, t1), pbk[:, 0:n])

                def proj_tm_pair(tlo):
                    vb = 512 if br == 0 else 256
                    pT2 = bank2()
                    for n_ in range(2):
                        t = tlo + n_
                        S.mm(pT2[:, n_, :], [(uT_v(kc, t, t + 1), wsb[:, kc, vb:vb + 512]) for kc in range(KC)])
                    gpair = V(gs.ap[:, tlo:tlo + 2, :].rearrange("p a b -> p (a b)"), gs_r[tlo:tlo + 2])
                    sg = sgs[(tlo // 2) % 2]
                    for n_ in range(2):
                        if not (DBG2 & 1):
                            S.copy("act", tl(vv, v_r, tlo + n_), pT2[:, n_, 0:DV])
                        if not (DBG2 & 2):
                            S.act(sg[:, n_, :], pT2[:, n_, DV:2 * DV], AF.Silu)
                    if not (DBG2 & 8):
                        S.tt("pool", gpair, flat(sg), flat(gh2), ALU.mult)
                    if not (DBG2 & 4):
                        ptk = bank_bf(bank())
                        S.transposes([(ptk[:, n_ * P:(n_ + 1) * P], kT_v(tlo + n_, tlo + n_ + 1)) for n_ in range(2)], ident_bf)
                        kpair = V(ktok.ap[:, tlo:tlo + 2, :].rearrange("p a b -> p (a b)"), ktok_r[tlo:tlo + 2])
                        S.copy("act", kpair, ptk[:, 0:2 * P])

                S.memset("dve", Sb2[0], 0.0)
                bst = {}
                first_of_group = {}
                for gi_, grp in enumerate(groups):
                    t0_, t1_ = grp
                    prs_ = [p_ for p_ in bwd_pairs if t0_ <= p_ < t1_]
                    first_of_group[prs_[0]] = (gi_, grp)

                def st_fm(k, tlo):
                    if tlo in first_of_group:
                        gi_, grp = first_of_group[tlo]
                        proj_fm(gi_, grp)

                def st_tm(k, tlo):
                    bst.setdefault(k, {})
                    proj_tm_pair(tlo)

                def st_tm_L(k, tlo):
                    proj_tm_pair(tlo)
                    bwd_L(k, tlo)

                def ktok_pair(tlo):
                    return V(ktok.ap[:, tlo:tlo + 2, :].rearrange("p a b -> p (a b)"), ktok_r[tlo:tlo + 2])

                def bwd_L(k, tlo):
                    c = bst.setdefault(k, {})
                    if br == 1:
                        c["gl"] = glr_load(tlo)

                def bwd_A(k, tlo):
                    c = bst[k]
                    if br == 1:
                        gl = c["gl"]
                        plg = bank()
                        for n_ in range(2):
                            S.mm(plg[:, n_ * P:(n_ + 1) * P], [(gl[:, n_ * P:(n_ + 1) * P], wup_h[0:17, P:2 * P])])
                        ex = ex_t[k % 2]
                        nl = nls_t[k % 2]
                        S.act(ex[:, 0:2 * P], plg[:, 0:2 * P], AF.Exp, scale=-1.0)
                        S.act(nl[:, 0:2 * P], ex[:, 0:2 * P], AF.Ln, scale=1.0, bias=one_t)
                        c["nl"] = nl

                def bwd_B(k, tlo):
                    c = bst[k]
                    kd = wk_kd[k % NB]
                    if br == 0:
                        S.act(kd, ktok_pair(tlo), AF.Copy, scale=wcol[:, 4 + h:5 + h])
                        c["a"] = [g128[:, 4 + h:5 + h], g128[:, 4 + h:5 + h]]
                    else:
                        nl = c["nl"]
                        prv = bank()
                        for n_ in range(2):
                            S.mm(prv[:, n_ * P:(n_ + 1) * P], [(cs(C_REVB), nl[:, n_ * P:(n_ + 1) * P])])
                        for n_ in range(2):
                            S.mm(prv[:, 2 * P + 2 * n_:2 * P + 2 * n_ + 2], [(nl[:, n_ * P:(n_ + 1) * P], cst[:, C_ONES, 0:2])])
                        e3 = e3_t[k % 2]
                        ab = af_t[k % NB]
                        S.act(e3, prv[:, 0:2 * P], AF.Exp, scale=-1.0 / 16.0)
                        S.act(ab[:, 0:4], prv[:, 2 * P:2 * P + 4], AF.Exp, scale=-1.0 / 16.0)
                        S.tt("pool", kd, ktok_pair(tlo), e3, ALU.mult)
                        c["a"] = [ab[:, 0:1], ab[:, 2:3]]
                    c["kd"] = kd

                def bwd_D(k, tlo):
                    c = bst.pop(k)
                    for j_, n_ in enumerate((1, 0)):
                        t = tlo + n_
                        kk = 2 * k + j_
                        need_sb = (t >= NCT) or ctx_out
                        Sb_cur = Sb2[kk % 2]
                        Sb_nxt = Sb2[(kk + 1) % 2]
                        pkv = bank()
                        S.mm(pkv[:, 0:DV], [(c["kd"][:, n_ * P:(n_ + 1) * P], tl(vv, v_r, t))])
                        S.stt(Sb_nxt, Sb_cur, c["a"][n_], pkv[:, 0:DV], ALU.mult, ALU.add)
                        if need_sb:
                            S.copy("pool", tl(sball, sb_r, t), Sb_cur)

                if DBG_STOP == "proj%d" % br:
                    pipeline([st_fm, None, st_tm], bwd_pairs)
                    S.barrier()
                    return nc
                pipeline([st_fm, None, st_tm if br == 0 else st_tm_L, bwd_A, bwd_B, bwd_D], bwd_pairs)
                if DBG_STOP in ("bwd%d" % br, "bwd%d:%d:%d" % (br, l, h)):
                    S.barrier()
                    return nc
                if h + 1 < NH:
                    load_head_w(h + 1)

                S.memset("dve", Sf2[0], 0.0)
                S.memset("pool", Sfb2[0], 0.0)
                odst, odst_r = (oret_s[l], oret_r[l]) if br == 0 else (ogla_s[l], ogla_r[l])
                fst = {}

                def fwd_L(k, tlo):
                    c = fst.setdefault(k, {})
                    if br == 1:
                        c["gl"] = glr_load(tlo)

                def fwd_A(k, tlo):
                    c = fst[k]
                    if br == 1:
                        gl = c["gl"]
                        plg = bank()
                        for n_ in range(2):
                            S.mm(plg[:, n_ * 2 * P:(n_ + 1) * 2 * P], [(gl[:, n_ * P:(n_ + 1) * P], wup_h[0:17, 0:2 * P])])
                        ex = ex_t[k % 2]
                        nl = nls_t[k % 2]
                        S.act(ex, plg, AF.Exp, scale=-1.0)
                        S.act(nl, ex, AF.Ln, scale=1.0, bias=one_t)
                        c["nl"] = nl

                def fwd_B(k, tlo):
                    c = fst[k]
                    need_out = (tlo >= NCT) or ctx_out
                    kb_ = k % NB
                    kd = wk_kd[kb_]
                    c["kd"] = kd
                    qp = qT_v(tlo, tlo + 2)
                    kp = kT_v(tlo, tlo + 2)
                    if br == 0:
                        S.act(kd, ktok_pair(tlo), AF.Copy, scale=wcol[:, h:h + 1])
                        c["a"] = [g128[:, h:h + 1], g128[:, h:h + 1]]
                        if need_out:
                            qdf = wk_q[0][kb_]
                            qdb = wk_q[1][kb_]
                            S.tt("pool", qdf, qp, GfP, ALU.mult)
                            S.tt("dve", qdb, qp, GbP, ALU.mult)
                            c["qf"], c["qb"] = qdf, qdb
                    else:
                        nl = c["nl"]
                        px = bank()
                        py_ = bank()
                        for n_ in range(2):
                            S.mm(px[:, n_ * P:(n_ + 1) * P], [(nl[:, n_ * 2 * P:n_ * 2 * P + P], cs(C_TRIF))])
                        for n_ in range(2):
                            S.mm(px[:, 2 * P + n_ * P:2 * P + (n_ + 1) * P], [(nl[:, n_ * 2 * P + P:(n_ + 1) * 2 * P], cs(C_TRIB))])
                        for n_ in range(2):
                            S.mm(py_[:, n_ * P:(n_ + 1) * P], [(cs(C_REVF), nl[:, n_ * 2 * P:n_ * 2 * P + P])])
                        E1 = E1_t[k % 2]
                        E2 = E2_t[k % 2]
                        e3f = e3_t[k % 2]
                        S.act(E1, px, AF.Exp, scale=-1.0 / 16.0)
                        S.act(e3f, py_[:, 0:2 * P], AF.Exp, scale=-1.0 / 16.0)
                        af = af_t[kb_]
                        S.act(af[:, 0:1], px[:, P - 1:P], AF.Exp, scale=-1.0 / 16.0)
                        S.act(af[:, 2:3], px[:, 2 * P - 1:2 * P], AF.Exp, scale=-1.0 / 16.0)
                        c["a"] = [af[:, 0:1], af[:, 2:3]]
                        S.tt("pool", kd, ktok_pair(tlo), e3f, ALU.mult)
                        if need_out:
                            S.act(E2, px, AF.Exp, scale=1.0 / 16.0)
                            qgf, qgb, krf, krb = (wk_q[i][kb_] for i in range(4))
                            S.tt("dve", qgf, qp, E1[:, 0:2 * P], ALU.mult)
                            S.tt("pool", qgb, qp, E1[:, 2 * P:4 * P], ALU.mult)
                            S.tt("pool", krf, kp, E2[:, 0:2 * P], ALU.mult)
                            S.tt("pool", krb, kp, E2[:, 2 * P:4 * P], ALU.mult)
                            c["qf"], c["qb"], c["krf"], c["krb"] = qgf, qgb, krf, krb

                def fwd_C(k, tlo):
                    c = fst[k]
                    need_out = (tlo >= NCT) or ctx_out
                    kb_ = k % NB
                    pkvb = pb[4 + (k % 2)]
                    c["pkv"] = pkvb
                    for n_ in range(2):
                        t = tlo + n_
                        if t < T - 1:
                            S.mm(pkvb[:, n_ * DV:(n_ + 1) * DV], [(c["kd"][:, n_ * P:(n_ + 1) * P], tl(vv, v_r, t))])
                    if not need_out:
                        return
                    ps = bank()
                    sm = wk_sm[kb_]
                    if br == 0:
                        for n_ in range(2):
                            S.mm(ps[:, n_ * P:(n_ + 1) * P], [(kT_v(tlo + n_, tlo + n_ + 1), qT_v(tlo + n_, tlo + n_ + 1))])
                        S.tt("dve", sm[:, 0:2 * P], ps[:, 0:2 * P], DtP, ALU.mult)
                    else:
                        for n_ in range(2):
                            S.mm(ps[:, n_ * P:(n_ + 1) * P], [(c["krf"][:, n_ * P:(n_ + 1) * P], c["qf"][:, n_ * P:(n_ + 1) * P])])
                        for n_ in range(2):
                            S.mm(ps[:, 2 * P + n_ * P:2 * P + (n_ + 1) * P], [(c["krb"][:, n_ * P:(n_ + 1) * P], c["qb"][:, n_ * P:(n_ + 1) * P])])
                        S.tt("dve", sm, ps, mask4, ALU.mult)
                    c["sm"] = sm

                def fwd_D(k, tlo):
                    c = fst[k]
                    need_out = (tlo >= NCT) or ctx_out
                    po = pb[6 + (k % 2)]
                    c["po"] = po
                    stt_ = st_t[k % 3]
                    c["st"] = stt_
                    pkvb = c["pkv"]
                    for n_ in range(2):
                        t = tlo + n_
                        kk = 2 * k + n_
                        ns_ = slice(n_ * P, (n_ + 1) * P)
                        if need_out:
                            sm = c["sm"]
                            prs_ = [(sm[:, ns_], tl(vv, v_r, t))]
                            if br == 1:
                                prs_.append((sm[:, 2 * P + n_ * P:2 * P + (n_ + 1) * P], tl(vv, v_r, t)))
                            prs_ += [(c["qf"][:, ns_], Sfb2[kk % 2]), (c["qb"][:, ns_], tl(sball, sb_r, t))]
                            S.mm(po[:, n_ * DV:(n_ + 1) * DV], prs_)
                        if t < T - 1:
                            S.stt(Sf2[(kk + 1) % 2], Sf2[kk % 2], c["a"][n_], pkvb[:, n_ * DV:(n_ + 1) * DV], ALU.mult, ALU.add)
                            S.copy("act", Sfb2[(kk + 1) % 2], Sf2[(kk + 1) % 2])

                def fwd_E(k, tlo):
                    c = fst.pop(k)
                    need_out = (tlo >= NCT) or ctx_out
                    if not need_out:
                        return
                    stt_ = c["st"]
                    po = c["po"]
                    on = on_t[k % 2]
                    ot = ot_t[k % 2]
                    for n_ in range(2):
                        pon = po[:, n_ * DV:(n_ + 1) * DV]
                        S.op("dve", lambda: nc.vector.bn_stats(out=stt_.ap[:, n_ * 6:n_ * 6 + 6], in_=pon.ap), [po], [stt_])
                        S.op("dve", lambda: nc.vector.bn_aggr(out=stt_.ap[:, 12 + 2 * n_:14 + 2 * n_], in_=stt_.ap[:, n_ * 6:n_ * 6 + 6]), [stt_], [stt_])
                    mv = V(stt_.ap[:, 12:16].rearrange("p (a b) -> p a b", a=2), stt_.res)
                    if br == 0:
                        S.act(stt_[:, 16:18], mv[:, :, 1], AF.Ln, scale=1.0, bias=eps_t)
                        S.act(stt_[:, 18:20], stt_[:, 16:18], AF.Exp, scale=-0.5)
                        S.stt(stt_[:, 20:22], mv[:, :, 0], -1.0, stt_[:, 18:20], ALU.mult, ALU.mult)
                        for n_ in range(2):
                            S.act(on[:, n_, :], po[:, n_ * DV:(n_ + 1) * DV], AF.Identity, scale=stt_[:, 18 + n_:19 + n_], bias=stt_[:, 20 + n_:21 + n_])
                    else:
                        S.tt("dve", stt_[:, 22:24], mv[:, :, 0], mv[:, :, 0], ALU.mult)
                        S.tt("dve", stt_[:, 22:24], stt_[:, 22:24], mv[:, :, 1], ALU.add)
                        S.act(stt_[:, 16:18], stt_[:, 22:24], AF.Ln, scale=1.0, bias=eps_t)
                        S.act(stt_[:, 18:20], stt_[:, 16:18], AF.Exp, scale=-0.5)
                        for n_ in range(2):
                            S.act(on[:, n_, :], po[:, n_ * DV:(n_ + 1) * DV], AF.Copy, scale=stt_[:, 18 + n_:19 + n_])
                    gpair = V(gs.ap[:, tlo:tlo + 2, :].rearrange("p a b -> p (a b)"), gs_r[tlo:tlo + 2])
                    S.tt("pool", flat(ot), flat(on), gpair, ALU.mult)
                    dst = V(odst[tlo * P:(tlo + 2) * P, h * DV:(h + 1) * DV].rearrange("(a p) c -> p a c", a=2), odst_r[tlo:tlo + 2])
                    S.dma("sp", dst, ot)

                pstate["mod"] = 4
                pipeline([fwd_L, fwd_A, fwd_B, fwd_C, fwd_D, fwd_E], list(range(0, T, 2)), order=[4, 5, 3, 2, 1, 0])
                pstate["mod"] = 8
                if DBG_STOP in ("fwd%d" % br, "fwd%d:%d:%d" % (br, l, h)):
                    S.barrier()
                    return nc

            S.barrier()
            A.off = pers_mark
            wbr = A.alloc([P, KC, D], BF16)
            wmm = A.alloc([P, KC, D], BF16)
            for kc in range(KC):
                S.dma("pool", wmm[:, kc, :], V(wm_d.ap[l, br, :, kc, :], []))
            for kc in range(KC):
                S.dma("pool", wbr[:, kc, :], V(wbr_d.ap[l, br, :, kc, :], []))
            ots = [A.alloc([P, D], BF16) for _ in range(3)]
            oTs = [A.alloc([P, KC, P], BF16) for _ in range(2)]
            sgt = [A.alloc([P, 512], F32) for _ in range(2)]
            osrc, osrc_r = (oret_s[l], oret_r[l]) if br == 0 else (ogla_s[l], ogla_r[l])

            def hsrc(t):
                if l == 0:
                    return V(ctx_d.ap[t * P:(t + 1) * P, :], []) if t < NCT else V(x_d.ap[(t - NCT) * P:(t - NCT + 1) * P, :], [])
                return rows(hctx_s[l - 1], t, hctx_r[l - 1][t]) if t < NCT else rows(hlat_s[l - 1], t - NCT, hlat_r[l - 1][t - NCT])

            if br == 0:
                m1t = [A.alloc([P, D], F32) for _ in range(2)]

                def dr_L(k, t):
                    S.dma("sp", ots[k % 3], rows(osrc, t, osrc_r[t]))

                def dr_A(k, t):
                    ot = ots[k % 3]
                    pt = bank_bf(bank())
                    S.transposes([(pt[:, kc * P:(kc + 1) * P], ot[:, kc * P:(kc + 1) * P]) for kc in range(KC)], ident_bf)
                    oT = oTs[k % 2]
                    S.copy("dve", V(oT.ap.rearrange("p a b -> p (a b)"), oT.res), pt)

                def dr_B(k, t):
                    oT = oTs[k % 2]
                    mt = m1t[k % 2]
                    for half in range(2):
                        py = bank()
                        pm = bank()
                        S.mm(pm, [(uT_v(kc, t, t + 1), wmm[:, kc, half * 512:(half + 1) * 512]) for kc in range(KC)])
                        S.mm(py, [(oT[:, kc, :], wbr[:, kc, half * 512:(half + 1) * 512]) for kc in range(KC)])
                        sg = sgt[half]
                        S.act(sg, pm, AF.Sigmoid)
                        S.tt("dve", mt[:, half * 512:(half + 1) * 512], py, sg, ALU.mult)
                    S.dma("sp", rows(m1_s[l], t, m1_r[l][t]), mt)

                pipeline([dr_L, dr_A, dr_B], out_tiles)
                if DBG_STOP == "pd0:%d" % l:
                    S.barrier()
                    return nc
            else:
                wout = A.alloc([P, KC, D], BF16)
                for kc in range(KC):
                    S.dma("pool", wout[:, kc, :], V(wout_d.ap[l, :, kc, :], []))
                ns = 2 if ctx_out else 1
                gate_bc = [A.alloc([P, D], F32) for _ in range(ns)]
                gate_mark = A.off
                bg = A.alloc([P, D], F32)
                S.dma("sp", bg, wrapd(bgate_d, l))
                srep = A.alloc([P, KC, P], F32)
                wg_ = [A.alloc([P, KC, 512], F32) for _ in range(2)]
                for blk in range(2):
                    S.dma("sp", wg_[blk], V(wada_d.ap[l, :, :, 2048 + blk * 512:2048 + (blk + 1) * 512], []))
                for s in range(ns):
                    for kc in range(KC):
                        S.ts("dve", srep[:, kc, :], cs(C_ONES), silu_col[:, kc * 2 + s:kc * 2 + s + 1], ALU.mult)
                    for blk in range(2):
                        pgt = bank()
                        S.mm(pgt, [(srep[:, kc, :], wg_[blk][:, kc, :]) for kc in range(KC)])
                        S.tt("dve", gate_bc[s][:, blk * 512:(blk + 1) * 512], pgt, bg[:, blk * 512:(blk + 1) * 512], ALU.add)
                S.barrier()
                A.off = gate_mark
                if last:
                    fg = A.alloc([P, D], F32)
                    S.dma("sp", fg, fgain_d)
                m1t = [A.alloc([P, D], F32) for _ in range(3)]
                hts2x = [A.alloc([P, D], F32) for _ in range(4)]
                mgs = [A.alloc([P, D], BF16) for _ in range(2)]
                mgT = [A.alloc([P, KC, P], BF16) for _ in range(2)]
                tmpf = [A.alloc([P, 512], F32) for _ in range(2)]
                hn_t = [A.alloc([P, D], F32) for _ in range(2)]
                junk2 = A.alloc([P, D], BF16)
                sst2 = [A.alloc([P, 4], F32) for _ in range(2)]

                def dg_L(k, t):
                    S.dma("sp", ots[k % 3], rows(osrc, t, osrc_r[t]))
                    S.dma("sp", hts2x[k % 4], hsrc(t))
                    S.dma("sp", m1t[k % 3], rows(m1_s[l], t, m1_r[l][t]))

                def dg_A(k, t):
                    ot = ots[k % 3]
                    pt = bank_bf(bank())
                    S.transposes([(pt[:, kc * P:(kc + 1) * P], ot[:, kc * P:(kc + 1) * P]) for kc in range(KC)], ident_bf)
                    oT = oTs[k % 2]
                    S.copy("dve", V(oT.ap.rearrange("p a b -> p (a b)"), oT.res), pt)

                def dg_B(k, t):
                    oT = oTs[k % 2]
                    mt = m1t[k % 3]
                    mg = mgs[k % 2]
                    for half in range(2):
                        hs = slice(half * 512, (half + 1) * 512)
                        py = bank()
                        pm = bank()
                        S.mm(pm, [(uT_v(kc, t, t + 1), wmm[:, kc, hs]) for kc in range(KC)])
                        S.mm(py, [(oT[:, kc, :], wbr[:, kc, hs]) for kc in range(KC)])
                        sg = sgt[half]
                        S.act(sg, pm, AF.Sigmoid)
                        tf = tmpf[half]
                        S.tt("dve", tf, py, sg, ALU.mult)
                        S.tt("pool", mg[:, hs], tf, mt[:, hs], ALU.add)

                def dg_C(k, t):
                    mg = mgs[k % 2]
                    pt2 = bank_bf(bank())
                    S.transposes([(pt2[:, kc * P:(kc + 1) * P], mg[:, kc * P:(kc + 1) * P]) for kc in range(KC)], ident_bf)
                    mT = mgT[k % 2]
                    S.copy("act", V(mT.ap.rearrange("p a b -> p (a b)"), mT.res), pt2)

                def dg_D(k, t):
                    s = 1 if t < NCT else 0
                    mT = mgT[k % 2]
                    ht = hts2x[k % 4]
                    hn = hn_t[k % 2]
                    for half in range(2):
                        hs = slice(half * 512, (half + 1) * 512)
                        po = bank()
                        S.mm(po, [(mT[:, kc, :], wout[:, kc, hs]) for kc in range(KC)])
                        tf = tmpf[half]
                        S.tt("dve", tf, po, gate_bc[s][:, hs], ALU.mult)
                        S.tt("pool", hn[:, hs], tf, ht[:, hs], ALU.add)
                    if last:
                        ss = sst2[k % 2]
                        S.act(junk2, hn, AF.Square, accum=ss[:, 0:1])
                        S.act(ss[:, 1:2], ss[:, 0:1], AF.Ln, scale=1.0 / D, bias=eps_t)
                        S.act(ss[:, 2:3], ss[:, 1:2], AF.Exp, scale=-0.5)
                        S.act(ht, hn, AF.Copy, scale=ss[:, 2:3])
                        S.tt("dve", hn, ht, fg, ALU.mult)
                        S.dma("sp", V(y_d.ap[(t - NCT) * P:(t - NCT + 1) * P, :], [y_r[t - NCT]]), hn)
                    else:
                        if t < NCT:
                            S.dma("sp", rows(hctx_s[l], t, hctx_r[l][t]), hn)
                        else:
                            S.dma("sp", rows(hlat_s[l], t - NCT, hlat_r[l][t - NCT]), hn)

                pipeline([dg_L, dg_A, dg_B, dg_C, dg_D], out_tiles)
                if DBG_STOP == "pd1:%d" % l:
                    S.barrier()
                    return nc
    S.barrier()
    build_program.stats = (S.ninstr, dict(S.cnt), A.peak * 4)
    return nc


def pipeline(stages, items, order=None):
    n = len(items)
    ns = len(stages)
    if order is None:
        order = list(range(ns - 1, -1, -1))
    for i in range(n + ns - 1):
        for s in order:
            k = i - s
            if 0 <= k < n and stages[s] is not None:
                stages[s](k, items[k])


def wrapd(v, l):
    return V(v.ap[l], [])


def _consts(NL):
    i = np.arange(P)
    ii = i[None, :].astype(np.float64)
    jj = i[:, None].astype(np.float64)
    c = np.zeros((14, P, P), np.float32)
    c[0] = np.eye(P)
    c[1] = (ii >= jj)
    c[2] = (jj > ii)
    c[3] = np.maximum(ii - jj, 0)
    c[4] = np.maximum(jj - ii, 0)
    c[5] = np.broadcast_to(ii + 1, (P, P))
    c[6] = np.broadcast_to(128 - ii, (P, P))
    c[7] = (jj <= ii)
    c[8] = (jj >= ii)
    c[9] = (jj > ii)
    c[10] = (jj < ii)
    c[11] = 1.0
    pm_ = _perm()
    c[13][pm_, np.arange(P)] = 1.0
    c[12, :, 0] = 127 - i
    c[12, :, 1] = i
    c[12, :, 2] = 128.0
    rows_ = NL * P // 64
    r_idx, c_idx = np.meshgrid(np.arange(rows_), np.arange(64), indexing="ij")
    r_idx = r_idx.reshape(-1).astype(np.float32)
    c_idx = c_idx.reshape(-1).astype(np.float32)
    n_freq = DK // 4
    inv_freq = (np.float32(10000.0) ** (-np.arange(n_freq, dtype=np.float32) / np.float32(n_freq))).astype(np.float32)
    ang_r = r_idx[:, None] * inv_freq
    ang_c = c_idx[:, None] * inv_freq
    ang = np.stack([ang_r, ang_r, ang_c, ang_c], axis=1).reshape(-1, DK)
    cos = np.cos(ang).astype(np.float32)
    sin = np.sin(ang).astype(np.float32)
    sign = np.ones(DK, np.float32)
    d = np.arange(DK)
    sign[(d // 32) % 2 == 0] = -1.0
    cosT = np.ascontiguousarray(cos.T)
    sinT = np.ascontiguousarray((sin * sign[None, :]).T)
    return c, cosT, sinT


_PERM = None


def _perm():
    d = np.arange(DK)
    a = d // 64
    pr = (d // 32) % 2
    f = d % 32
    return a * 64 + (1 - pr) * 32 + f


def _kc_layout(w):
    n = w.shape[1]
    return np.ascontiguousarray(w.reshape(KC, P, n).transpose(1, 0, 2))


def prepare_shared(inp, NL):
    L = inp["w_in"].shape[0]
    w_in = inp["w_in"]
    perm = _perm()
    cst, cosT, sinT = _consts(NL)
    sh = {"cst": cst, "cosT": cosT, "sinT": sinT}
    sh["wada"] = np.stack([_kc_layout(inp["w_ada"][l]) for l in range(L)])
    ba = inp["b_ada"]
    sh["badac"] = np.ascontiguousarray(ba[:, :2048].reshape(L, 16, P).transpose(0, 2, 1))
    sh["bgate"] = np.ascontiguousarray(np.broadcast_to(ba[:, None, 2048:], (L, P, D)))
    sh["ngain"] = np.ascontiguousarray(inp["norm_gain"].reshape(L, KC, P).transpose(0, 2, 1))
    sh["rgain"] = np.ascontiguousarray(np.broadcast_to(inp["ret_norm_gain"][:, None, :], (L, P, D)))
    sh["ggain"] = np.ascontiguousarray(np.broadcast_to(inp["gla_norm_gain"][:, None, :], (L, P, D)))
    sh["fgain"] = np.ascontiguousarray(np.broadcast_to(inp["final_norm_gain"][None, :], (P, D)))
    sh["rdec"] = np.ascontiguousarray(np.broadcast_to(inp["ret_decay"].reshape(L, 1, 8), (L, P, 8)))
    wup = np.zeros((L, 17, NH, 2, P), np.float32)
    for dr in range(2):
        wup[:, :16, :, dr, :] = inp["gla_w_up"][:, dr].reshape(L, 16, NH, P)
        wup[:, 16, :, dr, :] = inp["gla_b_up"][:, dr].reshape(L, NH, P)
    sh["wup"] = wup.reshape(L, 17, 1024)
    wret = np.zeros((L, NH, P, KC, 1024), np.float32)
    wgla = np.zeros((L, NH, P, KC, 768), np.float32)
    for l in range(L):
        for h in range(NH):
            q = w_in[l][:, h * 128:(h + 1) * 128]
            k = w_in[l][:, 512 + h * 128:512 + (h + 1) * 128]
            v = w_in[l][:, 1024 + h * 256:1024 + (h + 1) * 256]
            g = w_in[l][:, 2048 + h * 256:2048 + (h + 1) * 256]
            wret[l, h] = _kc_layout(np.concatenate([q, q[:, perm], k, k[:, perm], v, g], axis=1))
            q = w_in[l][:, 3072 + h * 128:3072 + (h + 1) * 128]
            k = w_in[l][:, 3584 + h * 128:3584 + (h + 1) * 128]
            v = w_in[l][:, 4096 + h * 256:4096 + (h + 1) * 256]
            g = w_in[l][:, 5120 + h * 256:5120 + (h + 1) * 256]
            wgla[l, h] = _kc_layout(np.concatenate([q, k, v, g], axis=1))
    sh["wret"] = wret
    sh["wgla"] = wgla
    sh["wglr"] = np.stack([_kc_layout(w_in[l][:, 6144:6160]) for l in range(L)])
    sh["wm"] = np.stack([np.stack([_kc_layout(w_in[l][:, 6160:7184]), _kc_layout(w_in[l][:, 7184:8208])]) for l in range(L)])
    sh["wbr"] = np.stack([np.stack([_kc_layout(inp["w_branch_ret"][l]), _kc_layout(inp["w_branch_gla"][l])]) for l in range(L)])
    sh["wout"] = np.stack([_kc_layout(inp["w_out"][l]) for l in range(L)])
    return sh


def run(inp, n_cores=None):
    inp = {k: np.asarray(v, dtype=np.float32) for k, v in inp.items()}
    B, SEQ, _ = inp["x"].shape
    CTX = inp["ctx"].shape[1]
    L = inp["w_in"].shape[0]
    NL = SEQ // P
    NCT = CTX // P
    nc = build_program(NCT, NL, L)
    sh = prepare_shared(inp, NL)
    in_maps = []
    for b in range(B):
        m = dict(sh)
        m["x"] = np.ascontiguousarray(inp["x"][b])
        m["ctx"] = np.ascontiguousarray(inp["ctx"][b])
        cc = np.stack([inp["c"][b].reshape(KC, P).T, inp["c_ctx"].reshape(KC, P).T], axis=2)
        m["ccol"] = np.ascontiguousarray(cc.reshape(P, 16))
        in_maps.append(m)
    res = run_bass_kernel_spmd(nc, in_maps, core_ids=list(range(B)))
    return np.stack([r["y"] for r in res.results], axis=0).astype(np.float32)


def kernel(**inputs):
    return run(inputs)
```

```python
import math
import numpy as np
import concourse.bass as bass
import concourse.mybir as mybir
from concourse.bass_utils import run_bass_kernel_spmd

F32 = mybir.dt.float32
BF16 = mybir.dt.bfloat16
AF = mybir.ActivationFunctionType
ALU = mybir.AluOpType

P = 128
D = 1024
KC = 8
NH = 4
DK = 128
DV = 256
EPS = 1e-6
QSCALE = DK ** -0.5
LNQ = math.log(QSCALE)
SAME_ENGINE_SYNC = True
import os
DBG_STOP = os.environ.get("KDBG", "")
DBG2 = int(os.environ.get("KDBG2", "0"))


class Res:
    __slots__ = ("w", "r")

    def __init__(self):
        self.w = None
        self.r = {}


class V:
    __slots__ = ("ap", "res")

    def __init__(self, ap, res):
        self.ap = ap
        self.res = tuple(res)

    def __getitem__(self, idx):
        return V(self.ap[idx], self.res)


def _resl(vs):
    out = []
    for v in vs:
        if isinstance(v, V):
            for r in v.res:
                if r not in out:
                    out.append(r)
    return out


class Sched:
    ENG = ("pe", "dve", "act", "pool", "sp")

    def __init__(self, nc, ndma=40):
        self.nc = nc
        self.e = {"pe": nc.tensor, "dve": nc.vector, "act": nc.scalar, "pool": nc.gpsimd, "sp": nc.sync}
        self.sem = {k: nc.alloc_semaphore("s_" + k) for k in self.ENG}
        self.cnt = {k: 0 for k in self.ENG}
        self.dsem = [nc.alloc_semaphore("d%d" % i) for i in range(ndma)]
        self.dcnt = [0] * ndma
        self.dnext = 0
        self.seen = {k: {} for k in self.ENG}
        self.ninstr = 0

    def _wait(self, eng, tok, raw=True):
        key, val = tok
        if key == eng and (eng == "pe" or not SAME_ENGINE_SYNC or not raw):
            return
        s = self.seen[eng]
        if s.get(key, 0) >= val:
            return
        s[key] = val
        sem = self.sem[key] if isinstance(key, str) else self.dsem[key]
        self.e[eng].wait_ge(sem, val)

    def deps(self, eng, reads, writes):
        for r in reads:
            if r.w is not None:
                self._wait(eng, r.w)
        for r in writes:
            if r.w is not None:
                self._wait(eng, r.w, raw=False)
            for k, v in r.r.items():
                self._wait(eng, (k, v), raw=False)

    def mark(self, tok, reads, writes):
        k, v = tok
        for r in reads:
            if r.r.get(k, 0) < v:
                r.r[k] = v
        for r in writes:
            r.w = tok
            r.r = {}

    def op(self, eng, ins, reads, writes):
        reads = _resl(reads)
        writes = _resl(writes)
        self.deps(eng, reads, writes)
        i = ins()
        self.cnt[eng] += 1
        i.then_inc(self.sem[eng], 1)
        self.mark((eng, self.cnt[eng]), reads, writes)
        self.ninstr += 1
        return i

    def dma(self, eng, out, in_, **kw):
        reads = _resl([in_])
        writes = _resl([out])
        k = self.dnext
        self.dnext = (self.dnext + 1) % len(self.dsem)
        if self.dcnt[k] > 0:
            self._wait(eng, (k, self.dcnt[k]))
        self.deps(eng, reads, writes)
        i = self.e[eng].dma_start(out=out.ap, in_=in_.ap, **kw)
        self.dcnt[k] += 16
        i.then_inc(self.dsem[k], 16)
        self.mark((k, self.dcnt[k]), reads, writes)
        self.ninstr += 1
        return i

    def barrier(self):
        for eng in self.ENG:
            for o in self.ENG:
                if o != eng and self.cnt[o] > 0:
                    self._wait(eng, (o, self.cnt[o]))
            for k in range(len(self.dsem)):
                if self.dcnt[k] > 0:
                    self._wait(eng, (k, self.dcnt[k]))

    def mm(self, out, pairs, extra_reads=()):
        nc = self.nc
        reads = []
        for a, b in pairs:
            reads += [a, b]
        reads = _resl(list(reads) + list(extra_reads))
        writes = _resl([out])
        self.deps("pe", reads, writes)
        n = len(pairs)
        for i, (a, b) in enumerate(pairs):
            ins = nc.tensor.matmul(out.ap, lhsT=a.ap, rhs=b.ap, start=(i == 0), stop=(i == n - 1))
            self.ninstr += 1
        self.cnt["pe"] += 1
        ins.then_inc(self.sem["pe"], 1)
        self.mark(("pe", self.cnt["pe"]), reads, writes)

    def transposes(self, items, ident):
        nc = self.nc
        reads = _resl([b for _, b in items] + [ident])
        writes = _resl([a for a, _ in items])
        self.deps("pe", reads, writes)
        for a, b in items:
            ins = nc.tensor.transpose(a.ap, b.ap, ident.ap)
            self.ninstr += 1
        self.cnt["pe"] += 1
        ins.then_inc(self.sem["pe"], 1)
        self.mark(("pe", self.cnt["pe"]), reads, writes)

    def act(self, out, in_, func, scale=1.0, bias=0.0, accum=None):
        nc = self.nc
        reads = [in_, scale, bias]
        writes = [out, accum]
        sc = scale.ap if isinstance(scale, V) else float(scale)
        bi = bias.ap if isinstance(bias, V) else float(bias)
        kw = {}
        if accum is not None:
            kw["accum_out"] = accum.ap
        return self.op("act", lambda: nc.scalar.activation(out=out.ap, in_=in_.ap, func=func, bias=bi, scale=sc, **kw),
                       reads, writes)

    def tt(self, eng, out, in0, in1, op):
        e = self.e[eng]
        return self.op(eng, lambda: e.tensor_tensor(out=out.ap, in0=in0.ap, in1=in1.ap, op=op), [in0, in1], [out])

    def ts(self, eng, out, in0, s1, op0, s2=None, op1=None):
        e = self.e[eng]
        a1 = s1.ap if isinstance(s1, V) else float(s1)
        a2 = None if s2 is None else (s2.ap if isinstance(s2, V) else float(s2))
        kw = {}
        if op1 is not None:
            kw["op1"] = op1
        return self.op(eng, lambda: e.tensor_scalar(out=out.ap, in0=in0.ap, scalar1=a1, scalar2=a2, op0=op0, **kw),
                       [in0, s1, s2], [out])

    def stt(self, out, in0, scalar, in1, op0, op1):
        nc = self.nc
        sc = scalar.ap if isinstance(scalar, V) else float(scalar)
        return self.op("dve", lambda: nc.vector.scalar_tensor_tensor(out=out.ap, in0=in0.ap, scalar=sc, in1=in1.ap,
                                                                     op0=op0, op1=op1), [in0, scalar, in1], [out])

    def copy(self, eng, out, in_):
        if eng == "act":
            return self.act(out, in_, AF.Copy)
        e = self.e[eng]
        return self.op(eng, lambda: e.tensor_copy(out=out.ap, in_=in_.ap), [in_], [out])

    def memset(self, eng, out, val):
        e = self.e[eng]
        return self.op(eng, lambda: e.memset(out.ap, val), [], [out])


class Arena:
    def __init__(self, nc, nbytes):
        self.words = nbytes // 4
        self.t = nc.alloc_sbuf_tensor("arena", [P, self.words], F32).ap()
        self.off = 0
        self.peak = 0

    def alloc(self, shape, dt=F32, res=None):
        parts = shape[0]
        n = 1
        for s in shape[1:]:
            n *= s
        esz = 4 if dt == F32 else 2
        nw = (n * esz + 3) // 4
        nw = (nw + 7) // 8 * 8
        assert self.off + nw <= self.words, "arena overflow: need %d have %d" % (self.off + nw, self.words)
        ap = self.t[0:parts, self.off:self.off + nw]
        self.off += nw
        self.peak = max(self.peak, self.off)
        if dt != F32:
            ap = ap.bitcast(dt)
        ap = ap[:, 0:n]
        if len(shape) == 3:
            ap = ap.rearrange("p (a b) -> p a b", a=shape[1])
        return V(ap, [res if res is not None else Res()])


def build_program(NCT, NL, L):
    T = NCT + NL
    NT = T * P
    nc = bass.Bass("TRN2", target_bir_lowering=False)
    S = Sched(nc)

    def din(name, shape):
        return V(nc.dram_tensor(name, list(shape), F32, kind="ExternalInput").ap(), [])

    x_d = din("x", [NL * P, D])
    ctx_d = din("ctx", [NCT * P, D])
    ccol_d = din("ccol", [P, 16])
    wada_d = din("wada", [L, P, KC, 3072])
    badac_d = din("badac", [L, P, 16])
    bgate_d = din("bgate", [L, P, D])
    ngain_d = din("ngain", [L, P, KC])
    rgain_d = din("rgain", [L, P, D])
    ggain_d = din("ggain", [L, P, D])
    fgain_d = din("fgain", [P, D])
    rdec_d = din("rdec", [L, P, 8])
    wup_d = din("wup", [L, 17, 1024])
    wret_d = din("wret", [L, NH, P, KC, 1024])
    wgla_d = din("wgla", [L, NH, P, KC, 768])
    wglr_d = din("wglr", [L, P, KC, 16])
    wm_d = din("wm", [L, 2, P, KC, 1024])
    wbr_d = din("wbr", [L, 2, P, KC, 1024])
    wout_d = din("wout", [L, P, KC, 1024])
    cst_d = din("cst", [14, P, P])
    cos_d = din("cosT", [P, NL * P])
    sin_d = din("sinT", [P, NL * P])
    y_d = V(nc.dram_tensor("y", [NL * P, D], F32, kind="ExternalOutput").ap(), [])
    y_r = [Res() for _ in range(NL)]

    def dscr(name, shape, dt):
        return nc.dram_tensor(name, list(shape), dt, kind="Internal").ap()

    hlat_s = [dscr("hlat%d" % l, [NL * P, D], F32) for l in range(L - 1)]
    hctx_s = [dscr("hctx%d" % l, [NCT * P, D], F32) for l in range(L - 1)]
    oret_s = [dscr("oret%d" % l, [NT, D], BF16) for l in range(L)]
    ogla_s = [dscr("ogla%d" % l, [NT, D], BF16) for l in range(L)]
    m1_s = [dscr("m1_%d" % l, [NT, D], F32) for l in range(L)]
    glr_s = [dscr("glr_%d" % l, [17, NT], BF16) for l in range(L)]
    hlat_r = [[Res() for _ in range(NL)] for _ in range(L - 1)]
    hctx_r = [[Res() for _ in range(NCT)] for _ in range(L - 1)]
    oret_r = [[Res() for _ in range(T)] for _ in range(L)]
    ogla_r = [[Res() for _ in range(T)] for _ in range(L)]
    m1_r = [[Res() for _ in range(T)] for _ in range(L)]

    def rows(ap, t, res):
        return V(ap[t * P:(t + 1) * P, :], [res])

    psum_all = nc.alloc_psum_tensor("psum_all", [P, 8 * 512], F32).ap()
    pb = [V(psum_all[:, i * 512:(i + 1) * 512], [Res()]) for i in range(8)]
    pstate = {"i": 0}

    def bank2():
        m_ = pstate.get("mod", 8)
        if pstate["i"] % 2 == 1:
            pstate["i"] = (pstate["i"] + 1) % m_
        if pstate["i"] >= m_:
            pstate["i"] = 0
        i0 = pstate["i"]
        pstate["i"] = (pstate["i"] + 2) % m_
        ap = psum_all[:, i0 * 512:(i0 + 2) * 512].rearrange("p (a b) -> p a b", a=2)
        return V(ap, [pb[i0].res[0], pb[i0 + 1].res[0]])

    def bank():
        m_ = pstate.get("mod", 8)
        if pstate["i"] >= m_:
            pstate["i"] = 0
        b = pb[pstate["i"]]
        pstate["i"] = (pstate["i"] + 1) % m_
        return b

    def bank_bf(b):
        return V(b.ap.bitcast(BF16), b.res)

    A = Arena(nc, 212800)
    uT = A.alloc([P, KC, NT], BF16)
    uT_r = [Res() for _ in range(T)]

    def uT_v(kc, t0, t1):
        return V(uT.ap[:, kc, t0 * P:t1 * P], uT_r[t0:t1])

    cst = A.alloc([P, 13, P], F32)
    ident_bf = A.alloc([P, P], BF16)
    ones_bf = A.alloc([P, P], BF16)
    C_ID, C_MF, C_MB, C_DPOS, C_DNEG, C_IPOS, C_INEG, C_TRIF, C_TRIB, C_REVF, C_REVB, C_ONES, C_COLC = range(13)

    def cs(i):
        return cst[:, i, :]

    silu_col = A.alloc([P, 16], F32)
    modc = A.alloc([P, 16, 2], F32)
    Amod = A.alloc([P, KC, 2], F32)
    small = A.alloc([P, 64], F32)
    LG = A.alloc([P, 8], F32)
    wcol = A.alloc([P, 8], F32)
    g128 = A.alloc([P, 8], F32)
    pers_mark = A.off

    S.dma("sp", cst, V(cst_d.ap[0:13].rearrange("c p f -> p c f"), []))
    S.dma("pool", ident_bf, cst_d[C_ID])
    S.dma("pool", ones_bf, cst_d[C_ONES])
    pm_bf = A.alloc([P, P], BF16)
    S.dma("pool", pm_bf, cst_d[13])
    S.dma("sp", silu_col, ccol_d)
    S.act(silu_col, silu_col, AF.Silu)

    def rstd_from(out, in_, scale, eng_tmp):
        S.act(eng_tmp, in_, AF.Ln, scale=scale, bias=eps_ap)
        S.act(out, eng_tmp, AF.Exp, scale=-0.5)

    eps_t = A.alloc([P, 1], F32)
    S.memset("dve", eps_t, EPS)
    one_t = A.alloc([P, 1], F32)
    S.memset("dve", one_t, 1.0)
    lnq_t = A.alloc([P, 1], F32)
    S.memset("dve", lnq_t, LNQ)
    eps_ap = eps_t
    pers_mark = A.off

    for l in range(L):
        last = (l == L - 1)
        ctx_out = not last
        out_tiles = list(range(T)) if ctx_out else list(range(NCT, T))
        A.off = pers_mark
        S.barrier()
        badac = A.alloc([P, 16], F32)
        ngain = A.alloc([P, KC], F32)
        S.dma("sp", badac, wrapd(badac_d, l))
        S.dma("sp", ngain, wrapd(ngain_d, l))
        wblk = [A.alloc([P, KC, 512], F32) for _ in range(2)]
        pmod = bank()
        for blk in range(4):
            wb_ = wblk[blk % 2]
            S.dma("sp", wb_, V(wada_d.ap[l, :, :, blk * 512:(blk + 1) * 512], []))
            for j in range(4):
                jj = blk * 4 + j
                S.mm(pmod[:, jj * 2:jj * 2 + 2],
                     [(wb_[:, kc, j * P:(j + 1) * P], silu_col[:, kc * 2:kc * 2 + 2]) for kc in range(KC)])
        pm3 = V(pmod.ap[:, 0:32].rearrange("p (a b) -> p a b", b=2), pmod.res)
        for s in range(2):
            S.tt("dve", modc[:, :, s], pm3[:, :, s], badac, ALU.add)
            S.stt(Amod[:, :, s], modc[:, 8:16, s], 1.0, ngain, ALU.add, ALU.mult)
        S.barrier()
        A.off = pers_mark
        hts = [A.alloc([P, D], F32) for _ in range(3)]
        xns = [A.alloc([P, D], BF16) for _ in range(2)]
        junk = A.alloc([P, D], BF16)
        sst = [A.alloc([P, 4], F32) for _ in range(2)]
        def pb_A(k, t):
            if l == 0:
                src = V(ctx_d.ap[t * P:(t + 1) * P, :], []) if t < NCT else V(x_d.ap[(t - NCT) * P:(t - NCT + 1) * P, :], [])
            else:
                src = rows(hctx_s[l - 1], t, hctx_r[l - 1][t]) if t < NCT else rows(hlat_s[l - 1], t - NCT, hlat_r[l - 1][t - NCT])
            S.dma("sp", hts[k % 3], src)

        def pb_B(k, t):
            ht = hts[k % 3]
            xn = xns[k % 2]
            ss = sst[k % 2]
            S.act(junk, ht, AF.Square, accum=ss[:, 0:1])
            S.act(ss[:, 1:2], ss[:, 0:1], AF.Ln, scale=1.0 / D, bias=eps_t)
            S.act(ss[:, 2:3], ss[:, 1:2], AF.Exp, scale=-0.5)
            S.act(xn, ht, AF.Copy, scale=ss[:, 2:3])

        def pb_C(k, t):
            s_ = 1 if t < NCT else 0
            xn = xns[k % 2]
            pt = bank_bf(bank())
            S.transposes([(pt[:, kc * P:(kc + 1) * P], xn[:, kc * P:(kc + 1) * P]) for kc in range(KC)], ident_bf)
            for kc in range(KC):
                o = V(uT.ap[:, kc, t * P:(t + 1) * P], [uT_r[t]])
                S.ts("dve", o, pt[:, kc * P:(kc + 1) * P], Amod[:, kc, s_:s_ + 1], ALU.mult, modc[:, kc, s_:s_ + 1], ALU.add)

        pipeline([pb_A, pb_B, pb_C], list(range(T)))

        groups = [(0, NCT)] if NCT > 0 else []
        g = NCT
        while g < T:
            groups.append((g, min(g + 4, T)))
            g += 4
        bwd_order = list(range(NCT - 1, -1, -1)) + list(range(T - 1, NCT - 1, -1))
        assert NCT % 2 == 0 and NL % 2 == 0
        bwd_pairs = list(range(NCT - 2, -1, -2)) + list(range(T - 2, NCT - 1, -2))

        for br in range(2):
            S.barrier()
            A.off = pers_mark
            gain_src = rgain_d if br == 0 else ggain_d
            if br == 0:
                rdec = A.alloc([P, 8], F32)
                S.dma("sp", rdec, wrapd(rdec_d, l))
                Dtot1 = [A.alloc([P, P], F32) for _ in range(NH)]
                Gf1 = [A.alloc([P, P], F32) for _ in range(NH)]
                Gb1 = [A.alloc([P, P], F32) for _ in range(NH)]
                DtP = A.alloc([P, 2 * P], F32)
                GfP = A.alloc([P, 2 * P], F32)
                GbP = A.alloc([P, 2 * P], F32)
                S.act(LG, rdec, AF.Exp)
                S.act(LG, LG, AF.Ln, scale=-1.0, bias=one_t)
                tmpa = A.alloc([P, P], F32)
                tmpb = A.alloc([P, P], F32)
                for h in range(NH):
                    S.act(tmpa, cs(C_DPOS), AF.Exp, scale=LG[:, h:h + 1], bias=lnq_t)
                    S.act(tmpb, cs(C_DNEG), AF.Exp, scale=LG[:, 4 + h:5 + h], bias=lnq_t)
                    S.tt("dve", tmpa, tmpa, cs(C_MF), ALU.mult)
                    S.tt("dve", tmpb, tmpb, cs(C_MB), ALU.mult)
                    S.tt("dve", Dtot1[h], tmpa, tmpb, ALU.add)
                    S.act(Gf1[h], cs(C_IPOS), AF.Exp, scale=LG[:, h:h + 1], bias=lnq_t)
                    S.act(Gb1[h], cs(C_INEG), AF.Exp, scale=LG[:, 4 + h:5 + h], bias=lnq_t)
                    S.act(wcol[:, h:h + 1], cst[:, C_COLC, 0:1], AF.Exp, scale=LG[:, h:h + 1])
                    S.act(wcol[:, 4 + h:5 + h], cst[:, C_COLC, 1:2], AF.Exp, scale=LG[:, 4 + h:5 + h])
                    S.act(g128[:, h:h + 1], cst[:, C_COLC, 2:3], AF.Exp, scale=LG[:, h:h + 1])
                    S.act(g128[:, 4 + h:5 + h], cst[:, C_COLC, 2:3], AF.Exp, scale=LG[:, 4 + h:5 + h])
            qT = A.alloc([P, NT], BF16)
            kT = A.alloc([P, NT], BF16)
            qT_r = [Res() for _ in range(T)]
            kT_r = [Res() for _ in range(T)]
            vv = A.alloc([P, T, DV], BF16)
            gs = A.alloc([P, T, DV], BF16)
            v_r = [Res() for _ in range(T)]
            gs_r = [Res() for _ in range(T)]
            sball = A.alloc([P, T, DV], BF16)
            sb_r = [Res() for _ in range(T)]
            ktok = A.alloc([P, T, DK], BF16)
            ktok_r = [Res() for _ in range(T)]
            if br == 0:
                NCOL = 1024
                wsrc = wret_d
            else:
                NCOL = 768
                wsrc = wgla_d
                glr_r = [Res() for _ in range(T)]
                glrg = [A.alloc([17, 512], BF16) for _ in range(1)] * 2
                glrt = [A.alloc([17, 2 * P], BF16) for _ in range(2)] * 2
                wup_hs = [A.alloc([17, 2 * P], BF16) for _ in range(2)]
                wglr = A.alloc([P, KC, 16], BF16)
                S.dma("pool", wglr, wrapd(wglr_d, l))
                S.memset("dve", glrg[0], 1.0)
                for gi_, (t0, t1) in enumerate(groups):
                    n = (t1 - t0) * P
                    pg = bank()
                    S.mm(pg[0:16, 0:n], [(wglr[:, kc, :], uT_v(kc, t0, t1)) for kc in range(KC)])
                    gg_ = glrg[gi_ % 2]
                    S.act(gg_[0:16, 0:n], pg[0:16, 0:n], AF.Copy)
                    S.dma("sp", V(glr_s[l][:, t0 * P:t1 * P], glr_r[t0:t1]), gg_[:, 0:n])
                glr_i = {"i": 0}

                def glr_load(t):
                    b_ = glrt[glr_i["i"] % 4]
                    glr_i["i"] += 1
                    S.dma("sp", b_, V(glr_s[l][:, t * P:(t + 2) * P], glr_r[t:t + 2]))
                    return b_
            wsb = A.alloc([P, KC, NCOL], BF16)
            Sf2 = [A.alloc([P, DV], F32) for _ in range(2)]
            Sb2 = [A.alloc([P, DV], F32) for _ in range(2)]
            Sfb2 = [A.alloc([P, DV], BF16) for _ in range(2)]
            NB = 3
            sgs = [A.alloc([P, 2, DV], F32) for _ in range(1)] * 2
            wk_kd = [A.alloc([P, 2 * P], BF16) for _ in range(NB)]
            wk_q = [[A.alloc([P, 2 * P], BF16) for _ in range(NB)] for _ in range(4 if br == 1 else 2)]
            wk_sm = [A.alloc([P, (4 if br == 1 else 2) * P], BF16) for _ in range(NB)]
            af_t = [A.alloc([P, 4], F32) for _ in range(NB)]
            if br == 0:
                costs = [A.alloc([P, 512], F32) for _ in range(1)] * 2
                sints = [A.alloc([P, 512], F32) for _ in range(1)] * 2
                rawb = A.alloc([P, 512], BF16)
                t1s = [A.alloc([P, 512], F32) for _ in range(1)]
                t2s = [A.alloc([P, 512], F32) for _ in range(1)]
            else:
                nls_t = [A.alloc([P, 4 * P], F32) for _ in range(2)] + [None]
                nls_t[2] = None
                ex_t = [A.alloc([P, 4 * P], F32) for _ in range(1)] * 2
                E1_t = [A.alloc([P, 4 * P], F32) for _ in range(1)] * 2
                E2_t = [A.alloc([P, 4 * P], F32) for _ in range(1)] * 2
                e3_t = [A.alloc([P, 2 * P], F32) for _ in range(1)] * 2
                mask4 = A.alloc([P, 4 * P], BF16)
                for i_ in range(4):
                    S.copy("dve", mask4[:, i_ * P:(i_ + 1) * P], cs(C_MF if i_ < 2 else C_MB))
            on_t = [A.alloc([P, 2, DV], F32) for _ in range(1)] * 2
            ot_t = [A.alloc([P, 2, DV], BF16) for _ in range(2)] + [None]
            ot_t[2] = ot_t[0]
            st_t = [A.alloc([P, 32], F32) for _ in range(3)]
            gh2s = [A.alloc([P, 2, DV], F32) for _ in range(1)] * 2

            def flat(v_):
                return V(v_.ap.rearrange("p a b -> p (a b)"), v_.res)

            def qT_v(t0, t1):
                return V(qT.ap[:, t0 * P:t1 * P], qT_r[t0:t1])

            def kT_v(t0, t1):
                return V(kT.ap[:, t0 * P:t1 * P], kT_r[t0:t1])

            def tl(arr, rl, t):
                return V(arr.ap[:, t, :], [rl[t]])

            def load_head_w(hh):
                for kc in range(KC):
                    S.dma("pool", wsb[:, kc, :], V(wsrc.ap[l, hh, :, kc, :], []))
                for n_ in range(2):
                    S.dma("sp", gh2s[hh % 2][:, n_, :], V(gain_src.ap[l, :, hh * DV:(hh + 1) * DV], []))
                if br == 1:
                    S.dma("pool", wup_hs[hh % 2], V(wup_d.ap[l, :, hh * 2 * P:(hh * 2 + 2) * P], []))

            load_head_w(0)
            if DBG_STOP == "pre%d" % br:
                S.barrier()
                return nc
            for h in range(NH):
                gh2 = gh2s[h % 2]
                if br == 1:
                    wup_h = wup_hs[h % 2]
                if br == 0:
                    for n_ in range(2):
                        S.copy("dve", DtP[:, n_ * P:(n_ + 1) * P], Dtot1[h])
                        S.copy("dve", GfP[:, n_ * P:(n_ + 1) * P], Gf1[h])
                        S.copy("dve", GbP[:, n_ * P:(n_ + 1) * P], Gb1[h])

                def proj_fm(gi, grp):
                    t0, t1 = grp
                    n = (t1 - t0) * P
                    is_ctx = t0 < NCT
                    if br == 0 and not is_ctx:
                        ct = costs[gi % 2]
                        st = sints[gi % 2]
                        lt0 = (t0 - NCT) * P
                        S.dma("sp", ct[:, 0:n], V(cos_d.ap[:, lt0:lt0 + n], []))
                        S.dma("sp", st[:, 0:n], V(sin_d.ap[:, lt0:lt0 + n], []))
                        for (blk, dstv) in ((0, qT_v(t0, t1)), (2, kT_v(t0, t1))):
                            pa = bank()
                            pbk = bank()
                            S.mm(pa[:, 0:n], [(wsb[:, kc, blk * P:(blk + 1) * P], uT_v(kc, t0, t1)) for kc in range(KC)])
                            S.copy("dve", rawb[:, 0:n], pa[:, 0:n])
                            S.mm(pbk[:, 0:n], [(pm_bf, rawb[:, 0:n])])
                            ta = t1s[0]
                            tb = t2s[0]
                            S.tt("dve", ta[:, 0:n], pa[:, 0:n], ct[:, 0:n], ALU.mult)
                            S.tt("dve", tb[:, 0:n], pbk[:, 0:n], st[:, 0:n], ALU.mult)
                            S.tt("pool", dstv, ta[:, 0:n], tb[:, 0:n], ALU.add)
                    else:
                        qb, kb = (0, 2) if br == 0 else (0, 1)
                        pa = bank()
                        S.mm(pa[:, 0:n], [(wsb[:, kc, qb * P:(qb + 1) * P], uT_v(kc, t0, t1)) for kc in range(KC)])
                        S.act(qT_v(t0, t1), pa[:, 0:n], AF.Copy, scale=(1.0 if br == 0 else QSCALE))
                        pbk = bank()
                        S.mm(pbk[:, 0:n], [(wsb[:, kc, kb * P:(kb + 1) * P], uT_v(kc, t0, t1)) for kc in range(KC)])
                        S.copy("dve", kT_v(t0, t1), pbk[:, 0:n])

                def proj_tm_pair(tlo):
                    vb = 512 if br == 0 else 256
                    pT2 = bank2()
                    for n_ in range(2):
                        t = tlo + n_
                        S.mm(pT2[:, n_, :], [(uT_v(kc, t, t + 1), wsb[:, kc, vb:vb + 512]) for kc in range(KC)])
                    gpair = V(gs.ap[:, tlo:tlo + 2, :].rearrange("p a b -> p (a b)"), gs_r[tlo:tlo + 2])
                    sg = sgs[(tlo // 2) % 2]
                    for n_ in range(2):
                        if not (DBG2 & 1):
                            S.copy("act", tl(vv, v_r, tlo + n_), pT2[:, n_, 0:DV])
                        if not (DBG2 & 2):
                            S.act(sg[:, n_, :], pT2[:, n_, DV:2 * DV], AF.Silu)
                    if not (DBG2 & 8):
                        S.tt("pool", gpair, flat(sg), flat(gh2), ALU.mult)
                    if not (DBG2 & 4):
                        ptk = bank_bf(bank())
                        S.transposes([(ptk[:, n_ * P:(n_ + 1) * P], kT_v(tlo + n_, tlo + n_ + 1)) for n_ in range(2)], ident_bf)
                        kpair = V(ktok.ap[:, tlo:tlo + 2, :].rearrange("p a b -> p (a b)"), ktok_r[tlo:tlo + 2])
                        S.copy("act", kpair, ptk[:, 0:2 * P])

                S.memset("dve", Sb2[0], 0.0)
                bst = {}
                first_of_group = {}
                for gi_, grp in enumerate(groups):
                    t0_, t1_ = grp
                    prs_ = [p_ for p_ in bwd_pairs if t0_ <= p_ < t1_]
                    first_of_group[prs_[0]] = (gi_, grp)

                def st_fm(k, tlo):
                    if tlo in first_of_group:
                        gi_, grp = first_of_group[tlo]
                        proj_fm(gi_, grp)

                def st_tm(k, tlo):
                    bst.setdefault(k, {})
                    proj_tm_pair(tlo)

                def st_tm_L(k, tlo):
                    proj_tm_pair(tlo)
                    bwd_L(k, tlo)

                def ktok_pair(tlo):
                    return V(ktok.ap[:, tlo:tlo + 2, :].rearrange("p a b -> p (a b)"), ktok_r[tlo:tlo + 2])

                def bwd_L(k, tlo):
                    c = bst.setdefault(k, {})
                    if br == 1:
                        c["gl"] = glr_load(tlo)

                def bwd_A(k, tlo):
                    c = bst[k]
                    if br == 1:
                        gl = c["gl"]
                        plg = bank()
                        for n_ in range(2):
                            S.mm(plg[:, n_ * P:(n_ + 1) * P], [(gl[:, n_ * P:(n_ + 1) * P], wup_h[0:17, P:2 * P])])
                        ex = ex_t[k % 2]
                        nl = nls_t[k % 2]
                        S.act(ex[:, 0:2 * P], plg[:, 0:2 * P], AF.Exp, scale=-1.0)
                        S.act(nl[:, 0:2 * P], ex[:, 0:2 * P], AF.Ln, scale=1.0, bias=one_t)
                        c["nl"] = nl

                def bwd_B(k, tlo):
                    c = bst[k]
                    kd = wk_kd[k % NB]
                    if br == 0:
                        S.act(kd, ktok_pair(tlo), AF.Copy, scale=wcol[:, 4 + h:5 + h])
                        c["a"] = [g128[:, 4 + h:5 + h], g128[:, 4 + h:5 + h]]
                    else:
                        nl = c["nl"]
                        prv = bank()
                        for n_ in range(2):
                            S.mm(prv[:, n_ * P:(n_ + 1) * P], [(cs(C_REVB), nl[:, n_ * P:(n_ + 1) * P])])
                        for n_ in range(2):
                            S.mm(prv[:, 2 * P + 2 * n_:2 * P + 2 * n_ + 2], [(nl[:, n_ * P:(n_ + 1) * P], cst[:, C_ONES, 0:2])])
                        e3 = e3_t[k % 2]
                        ab = af_t[k % NB]
                        S.act(e3, prv[:, 0:2 * P], AF.Exp, scale=-1.0 / 16.0)
                        S.act(ab[:, 0:4], prv[:, 2 * P:2 * P + 4], AF.Exp, scale=-1.0 / 16.0)
                        S.tt("pool", kd, ktok_pair(tlo), e3, ALU.mult)
                        c["a"] = [ab[:, 0:1], ab[:, 2:3]]
                    c["kd"] = kd

                def bwd_D(k, tlo):
                    c = bst.pop(k)
                    for j_, n_ in enumerate((1, 0)):
                        t = tlo + n_
                        kk = 2 * k + j_
                        need_sb = (t >= NCT) or ctx_out
                        Sb_cur = Sb2[kk % 2]
                        Sb_nxt = Sb2[(kk + 1) % 2]
                        pkv = bank()
                        S.mm(pkv[:, 0:DV], [(c["kd"][:, n_ * P:(n_ + 1) * P], tl(vv, v_r, t))])
                        S.stt(Sb_nxt, Sb_cur, c["a"][n_], pkv[:, 0:DV], ALU.mult, ALU.add)
                        if need_sb:
                            S.copy("pool", tl(sball, sb_r, t), Sb_cur)

                if DBG_STOP == "proj%d" % br:
                    pipeline([st_fm, None, st_tm], bwd_pairs)
                    S.barrier()
                    return nc
                pipeline([st_fm, None, st_tm if br == 0 else st_tm_L, bwd_A, bwd_B, bwd_D], bwd_pairs)
                if DBG_STOP in ("bwd%d" % br, "bwd%d:%d:%d" % (br, l, h)):
                    S.barrier()
                    return nc
                if h + 1 < NH:
                    load_head_w(h + 1)

                S.memset("dve", Sf2[0], 0.0)
                S.memset("pool", Sfb2[0], 0.0)
                odst, odst_r = (oret_s[l], oret_r[l]) if br == 0 else (ogla_s[l], ogla_r[l])
                fst = {}

                def fwd_L(k, tlo):
                    c = fst.setdefault(k, {})
                    if br == 1:
                        c["gl"] = glr_load(tlo)

                def fwd_A(k, tlo):
                    c = fst[k]
                    if br == 1:
                        gl = c["gl"]
                        plg = bank()
                        for n_ in range(2):
                            S.mm(plg[:, n_ * 2 * P:(n_ + 1) * 2 * P], [(gl[:, n_ * P:(n_ + 1) * P], wup_h[0:17, 0:2 * P])])
                        ex = ex_t[k % 2]
                        nl = nls_t[k % 2]
                        S.act(ex, plg, AF.Exp, scale=-1.0)
                        S.act(nl, ex, AF.Ln, scale=1.0, bias=one_t)
                        c["nl"] = nl

                def fwd_B(k, tlo):
                    c = fst[k]
                    need_out = (tlo >= NCT) or ctx_out
                    kb_ = k % NB
                    kd = wk_kd[kb_]
                    c["kd"] = kd
                    qp = qT_v(tlo, tlo + 2)
                    kp = kT_v(tlo, tlo + 2)
                    if br == 0:
                        S.act(kd, ktok_pair(tlo), AF.Copy, scale=wcol[:, h:h + 1])
                        c["a"] = [g128[:, h:h + 1], g128[:, h:h + 1]]
                        if need_out:
                            qdf = wk_q[0][kb_]
                            qdb = wk_q[1][kb_]
                            S.tt("pool", qdf, qp, GfP, ALU.mult)
                            S.tt("dve", qdb, qp, GbP, ALU.mult)
                            c["qf"], c["qb"] = qdf, qdb
                    else:
                        nl = c["nl"]
                        px = bank()
                        py_ = bank()
                        for n_ in range(2):
                            S.mm(px[:, n_ * P:(n_ + 1) * P], [(nl[:, n_ * 2 * P:n_ * 2 * P + P], cs(C_TRIF))])
                        for n_ in range(2):
                            S.mm(px[:, 2 * P + n_ * P:2 * P + (n_ + 1) * P], [(nl[:, n_ * 2 * P + P:(n_ + 1) * 2 * P], cs(C_TRIB))])
                        for n_ in range(2):
                            S.mm(py_[:, n_ * P:(n_ + 1) * P], [(cs(C_REVF), nl[:, n_ * 2 * P:n_ * 2 * P + P])])
                        E1 = E1_t[k % 2]
                        E2 = E2_t[k % 2]
                        e3f = e3_t[k % 2]
                        S.act(E1, px, AF.Exp, scale=-1.0 / 16.0)
                        S.act(e3f, py_[:, 0:2 * P], AF.Exp, scale=-1.0 / 16.0)
                        af = af_t[kb_]
                        S.act(af[:, 0:1], px[:, P - 1:P], AF.Exp, scale=-1.0 / 16.0)
                        S.act(af[:, 2:3], px[:, 2 * P - 1:2 * P], AF.Exp, scale=-1.0 / 16.0)
                        c["a"] = [af[:, 0:1], af[:, 2:3]]
                        S.tt("pool", kd, ktok_pair(tlo), e3f, ALU.mult)
                        if need_out:
                            S.act(E2, px, AF.Exp, scale=1.0 / 16.0)
                            qgf, qgb, krf, krb = (wk_q[i][kb_] for i in range(4))
                            S.tt("dve", qgf, qp, E1[:, 0:2 * P], ALU.mult)
                            S.tt("pool", qgb, qp, E1[:, 2 * P:4 * P], ALU.mult)
                            S.tt("pool", krf, kp, E2[:, 0:2 * P], ALU.mult)
                            S.tt("pool", krb, kp, E2[:, 2 * P:4 * P], ALU.mult)
                            c["qf"], c["qb"], c["krf"], c["krb"] = qgf, qgb, krf, krb

                def fwd_C(k, tlo):
                    c = fst[k]
                    need_out = (tlo >= NCT) or ctx_out
                    kb_ = k % NB
                    pkvb = pb[4 + (k % 2)]
                    c["pkv"] = pkvb
                    for n_ in range(2):
                        t = tlo + n_
                        if t < T - 1:
                            S.mm(pkvb[:, n_ * DV:(n_ + 1) * DV], [(c["kd"][:, n_ * P:(n_ + 1) * P], tl(vv, v_r, t))])
                    if not need_out:
                        return
                    ps = bank()
                    sm = wk_sm[kb_]
                    if br == 0:
                        for n_ in range(2):
                            S.mm(ps[:, n_ * P:(n_ + 1) * P], [(kT_v(tlo + n_, tlo + n_ + 1), qT_v(tlo + n_, tlo + n_ + 1))])
                        S.tt("dve", sm[:, 0:2 * P], ps[:, 0:2 * P], DtP, ALU.mult)
                    else:
                        for n_ in range(2):
                            S.mm(ps[:, n_ * P:(n_ + 1) * P], [(c["krf"][:, n_ * P:(n_ + 1) * P], c["qf"][:, n_ * P:(n_ + 1) * P])])
                        for n_ in range(2):
                            S.mm(ps[:, 2 * P + n_ * P:2 * P + (n_ + 1) * P], [(c["krb"][:, n_ * P:(n_ + 1) * P], c["qb"][:, n_ * P:(n_ + 1) * P])])
                        S.tt("dve", sm, ps, mask4, ALU.mult)
                    c["sm"] = sm

                def fwd_D(k, tlo):
                    c = fst[k]
                    need_out = (tlo >= NCT) or ctx_out
                    po = pb[6 + (k % 2)]
                    c["po"] = po
                    stt_ = st_t[k % 3]
                    c["st"] = stt_
                    pkvb = c["pkv"]
                    for n_ in range(2):
                        t = tlo + n_
                        kk = 2 * k + n_
                        ns_ = slice(n_ * P, (n_ + 1) * P)
                        if need_out:
                            sm = c["sm"]
                            prs_ = [(sm[:, ns_], tl(vv, v_r, t))]
                            if br == 1:
                                prs_.append((sm[:, 2 * P + n_ * P:2 * P + (n_ + 1) * P], tl(vv, v_r, t)))
                            prs_ += [(c["qf"][:, ns_], Sfb2[kk % 2]), (c["qb"][:, ns_], tl(sball, sb_r, t))]
                            S.mm(po[:, n_ * DV:(n_ + 1) * DV], prs_)
                        if t < T - 1:
                            S.stt(Sf2[(kk + 1) % 2], Sf2[kk % 2], c["a"][n_], pkvb[:, n_ * DV:(n_ + 1) * DV], ALU.mult, ALU.add)
                            S.copy("act", Sfb2[(kk + 1) % 2], Sf2[(kk + 1) % 2])

                def fwd_E(k, tlo):
                    c = fst.pop(k)
                    need_out = (tlo >= NCT) or ctx_out
                    if not need_out:
                        return
                    stt_ = c["st"]
                    po = c["po"]
                    on = on_t[k % 2]
                    ot = ot_t[k % 2]
                    for n_ in range(2):
                        pon = po[:, n_ * DV:(n_ + 1) * DV]
                        S.op("dve", lambda: nc.vector.bn_stats(out=stt_.ap[:, n_ * 6:n_ * 6 + 6], in_=pon.ap), [po], [stt_])
                        S.op("dve", lambda: nc.vector.bn_aggr(out=stt_.ap[:, 12 + 2 * n_:14 + 2 * n_], in_=stt_.ap[:, n_ * 6:n_ * 6 + 6]), [stt_], [stt_])
                    mv = V(stt_.ap[:, 12:16].rearrange("p (a b) -> p a b", a=2), stt_.res)
                    if br == 0:
                        S.act(stt_[:, 16:18], mv[:, :, 1], AF.Ln, scale=1.0, bias=eps_t)
                        S.act(stt_[:, 18:20], stt_[:, 16:18], AF.Exp, scale=-0.5)
                        S.stt(stt_[:, 20:22], mv[:, :, 0], -1.0, stt_[:, 18:20], ALU.mult, ALU.mult)
                        for n_ in range(2):
                            S.act(on[:, n_, :], po[:, n_ * DV:(n_ + 1) * DV], AF.Identity, scale=stt_[:, 18 + n_:19 + n_], bias=stt_[:, 20 + n_:21 + n_])
                    else:
                        S.tt("dve", stt_[:, 22:24], mv[:, :, 0], mv[:, :, 0], ALU.mult)
                        S.tt("dve", stt_[:, 22:24], stt_[:, 22:24], mv[:, :, 1], ALU.add)
                        S.act(stt_[:, 16:18], stt_[:, 22:24], AF.Ln, scale=1.0, bias=eps_t)
                        S.act(stt_[:, 18:20], stt_[:, 16:18], AF.Exp, scale=-0.5)
                        for n_ in range(2):
                            S.stt(ot[:, n_, :], po[:, n_ * DV:(n_ + 1) * DV], stt_[:, 18 + n_:19 + n_],
                                  tl(gs, gs_r, tlo + n_), ALU.mult, ALU.mult)
                    if br == 0:
                        gpair = V(gs.ap[:, tlo:tlo + 2, :].rearrange("p a b -> p (a b)"), gs_r[tlo:tlo + 2])
                        S.tt("pool", flat(ot), flat(on), gpair, ALU.mult)
                    dst = V(odst[tlo * P:(tlo + 2) * P, h * DV:(h + 1) * DV].rearrange("(a p) c -> p a c", a=2), odst_r[tlo:tlo + 2])
                    S.dma("sp", dst, ot)

                pstate["mod"] = 4
                pipeline([fwd_L, fwd_A, fwd_B, fwd_C, fwd_D, fwd_E], list(range(0, T, 2)), order=[4, 5, 3, 2, 1, 0])
                pstate["mod"] = 8
                if DBG_STOP in ("fwd%d" % br, "fwd%d:%d:%d" % (br, l, h)):
                    S.barrier()
                    return nc

            S.barrier()
            A.off = pers_mark
            wbr = A.alloc([P, KC, D], BF16)
            wmm = A.alloc([P, KC, D], BF16)
            for kc in range(KC):
                S.dma("pool", wbr[:, kc, :], V(wbr_d.ap[l, br, :, kc, :], []))
                S.dma("pool", wmm[:, kc, :], V(wm_d.ap[l, br, :, kc, :], []))
            ots = [A.alloc([P, D], BF16) for _ in range(3)]
            oTs = [A.alloc([P, KC, P], BF16) for _ in range(2)]
            sgt = [A.alloc([P, 512], F32) for _ in range(2)]
            osrc, osrc_r = (oret_s[l], oret_r[l]) if br == 0 else (ogla_s[l], ogla_r[l])

            def hsrc(t):
                if l == 0:
                    return V(ctx_d.ap[t * P:(t + 1) * P, :], []) if t < NCT else V(x_d.ap[(t - NCT) * P:(t - NCT + 1) * P, :], [])
                return rows(hctx_s[l - 1], t, hctx_r[l - 1][t]) if t < NCT else rows(hlat_s[l - 1], t - NCT, hlat_r[l - 1][t - NCT])

            if br == 0:
                m1t = [A.alloc([P, D], F32) for _ in range(2)]

                def dr_L(k, t):
                    S.dma("sp", ots[k % 3], rows(osrc, t, osrc_r[t]))

                def dr_A(k, t):
                    ot = ots[k % 3]
                    pt = bank_bf(bank())
                    S.transposes([(pt[:, kc * P:(kc + 1) * P], ot[:, kc * P:(kc + 1) * P]) for kc in range(KC)], ident_bf)
                    oT = oTs[k % 2]
                    S.copy("dve", V(oT.ap.rearrange("p a b -> p (a b)"), oT.res), pt)

                def dr_B(k, t):
                    oT = oTs[k % 2]
                    mt = m1t[k % 2]
                    for half in range(2):
                        py = bank()
                        pm = bank()
                        S.mm(pm, [(uT_v(kc, t, t + 1), wmm[:, kc, half * 512:(half + 1) * 512]) for kc in range(KC)])
                        S.mm(py, [(oT[:, kc, :], wbr[:, kc, half * 512:(half + 1) * 512]) for kc in range(KC)])
                        sg = sgt[half]
                        S.act(sg, pm, AF.Sigmoid)
                        S.tt("dve", mt[:, half * 512:(half + 1) * 512], py, sg, ALU.mult)
                    S.dma("sp", rows(m1_s[l], t, m1_r[l][t]), mt)

                pipeline([dr_L, dr_A, dr_B], out_tiles)
                if DBG_STOP == "pd0:%d" % l:
                    S.barrier()
                    return nc
            else:
                wout = A.alloc([P, KC, D], BF16)
                for kc in range(KC):
                    S.dma("pool", wout[:, kc, :], V(wout_d.ap[l, :, kc, :], []))
                ns = 2 if ctx_out else 1
                gate_bc = [A.alloc([P, D], F32) for _ in range(ns)]
                gate_mark = A.off
                bg = A.alloc([P, D], F32)
                S.dma("sp", bg, wrapd(bgate_d, l))
                srep = A.alloc([P, KC, P], F32)
                wg_ = [A.alloc([P, KC, 512], F32) for _ in range(2)]
                for blk in range(2):
                    S.dma("sp", wg_[blk], V(wada_d.ap[l, :, :, 2048 + blk * 512:2048 + (blk + 1) * 512], []))
                for s in range(ns):
                    for kc in range(KC):
                        S.ts("dve", srep[:, kc, :], cs(C_ONES), silu_col[:, kc * 2 + s:kc * 2 + s + 1], ALU.mult)
                    for blk in range(2):
                        pgt = bank()
                        S.mm(pgt, [(srep[:, kc, :], wg_[blk][:, kc, :]) for kc in range(KC)])
                        S.tt("dve", gate_bc[s][:, blk * 512:(blk + 1) * 512], pgt, bg[:, blk * 512:(blk + 1) * 512], ALU.add)
                S.barrier()
                A.off = gate_mark
                if last:
                    fg = A.alloc([P, D], F32)
                    S.dma("sp", fg, fgain_d)
                m1t = [A.alloc([P, D], F32) for _ in range(3)]
                hts2x = [A.alloc([P, D], F32) for _ in range(4)]
                mgs = [A.alloc([P, D], BF16) for _ in range(2)]
                mgT = [A.alloc([P, KC, P], BF16) for _ in range(2)]
                tmpf = [A.alloc([P, 512], F32) for _ in range(2)]
                hn_t = [A.alloc([P, D], F32) for _ in range(2)]
                junk2 = A.alloc([P, D], BF16)
                sst2 = [A.alloc([P, 4], F32) for _ in range(2)]

                def dg_L(k, t):
                    S.dma("sp", ots[k % 3], rows(osrc, t, osrc_r[t]))
                    S.dma("sp", hts2x[k % 4], hsrc(t))
                    S.dma("sp", m1t[k % 3], rows(m1_s[l], t, m1_r[l][t]))

                def dg_A(k, t):
                    ot = ots[k % 3]
                    pt = bank_bf(bank())
                    S.transposes([(pt[:, kc * P:(kc + 1) * P], ot[:, kc * P:(kc + 1) * P]) for kc in range(KC)], ident_bf)
                    oT = oTs[k % 2]
                    S.copy("dve", V(oT.ap.rearrange("p a b -> p (a b)"), oT.res), pt)

                def dg_B(k, t):
                    oT = oTs[k % 2]
                    mt = m1t[k % 3]
                    mg = mgs[k % 2]
                    for half in range(2):
                        hs = slice(half * 512, (half + 1) * 512)
                        py = bank()
                        pm = bank()
                        S.mm(pm, [(uT_v(kc, t, t + 1), wmm[:, kc, hs]) for kc in range(KC)])
                        S.mm(py, [(oT[:, kc, :], wbr[:, kc, hs]) for kc in range(KC)])
                        sg = sgt[half]
                        S.act(sg, pm, AF.Sigmoid)
                        tf = tmpf[half]
                        S.tt("dve", tf, py, sg, ALU.mult)
                        S.tt("pool", mg[:, hs], tf, mt[:, hs], ALU.add)

                def dg_C(k, t):
                    mg = mgs[k % 2]
                    pt2 = bank_bf(bank())
                    S.transposes([(pt2[:, kc * P:(kc + 1) * P], mg[:, kc * P:(kc + 1) * P]) for kc in range(KC)], ident_bf)
                    mT = mgT[k % 2]
                    S.copy("act", V(mT.ap.rearrange("p a b -> p (a b)"), mT.res), pt2)

                def dg_D(k, t):
                    s = 1 if t < NCT else 0
                    mT = mgT[k % 2]
                    ht = hts2x[k % 4]
                    hn = hn_t[k % 2]
                    for half in range(2):
                        hs = slice(half * 512, (half + 1) * 512)
                        po = bank()
                        S.mm(po, [(mT[:, kc, :], wout[:, kc, hs]) for kc in range(KC)])
                        tf = tmpf[half]
                        S.tt("dve", tf, po, gate_bc[s][:, hs], ALU.mult)
                        S.tt("pool", hn[:, hs], tf, ht[:, hs], ALU.add)
                    if last:
                        ss = sst2[k % 2]
                        S.act(junk2, hn, AF.Square, accum=ss[:, 0:1])
                        S.act(ss[:, 1:2], ss[:, 0:1], AF.Ln, scale=1.0 / D, bias=eps_t)
                        S.act(ss[:, 2:3], ss[:, 1:2], AF.Exp, scale=-0.5)
                        S.act(ht, hn, AF.Copy, scale=ss[:, 2:3])
                        S.tt("dve", hn, ht, fg, ALU.mult)
                        S.dma("sp", V(y_d.ap[(t - NCT) * P:(t - NCT + 1) * P, :], [y_r[t - NCT]]), hn)
                    else:
                        if t < NCT:
                            S.dma("sp", rows(hctx_s[l], t, hctx_r[l][t]), hn)
                        else:
                            S.dma("sp", rows(hlat_s[l], t - NCT, hlat_r[l][t - NCT]), hn)

                pipeline([dg_L, dg_A, dg_B, dg_C, dg_D], out_tiles)
                if DBG_STOP == "pd1:%d" % l:
                    S.barrier()
                    return nc
    S.barrier()
    build_program.stats = (S.ninstr, dict(S.cnt), A.peak * 4)
    return nc


def pipeline(stages, items, order=None):
    n = len(items)
    ns = len(stages)
    if order is None:
        order = list(range(ns - 1, -1, -1))
    for i in range(n + ns - 1):
        for s in order:
            k = i - s
            if 0 <= k < n and stages[s] is not None:
                stages[s](k, items[k])


def wrapd(v, l):
    return V(v.ap[l], [])


def _consts(NL):
    i = np.arange(P)
    ii = i[None, :].astype(np.float64)
    jj = i[:, None].astype(np.float64)
    c = np.zeros((14, P, P), np.float32)
    c[0] = np.eye(P)
    c[1] = (ii >= jj)
    c[2] = (jj > ii)
    c[3] = np.maximum(ii - jj, 0)
    c[4] = np.maximum(jj - ii, 0)
    c[5] = np.broadcast_to(ii + 1, (P, P))
    c[6] = np.broadcast_to(128 - ii, (P, P))
    c[7] = (jj <= ii)
    c[8] = (jj >= ii)
    c[9] = (jj > ii)
    c[10] = (jj < ii)
    c[11] = 1.0
    pm_ = _perm()
    c[13][pm_, np.arange(P)] = 1.0
    c[12, :, 0] = 127 - i
    c[12, :, 1] = i
    c[12, :, 2] = 128.0
    rows_ = NL * P // 64
    r_idx, c_idx = np.meshgrid(np.arange(rows_), np.arange(64), indexing="ij")
    r_idx = r_idx.reshape(-1).astype(np.float32)
    c_idx = c_idx.reshape(-1).astype(np.float32)
    n_freq = DK // 4
    inv_freq = (np.float32(10000.0) ** (-np.arange(n_freq, dtype=np.float32) / np.float32(n_freq))).astype(np.float32)
    ang_r = r_idx[:, None] * inv_freq
    ang_c = c_idx[:, None] * inv_freq
    ang = np.stack([ang_r, ang_r, ang_c, ang_c], axis=1).reshape(-1, DK)
    cos = np.cos(ang).astype(np.float32)
    sin = np.sin(ang).astype(np.float32)
    sign = np.ones(DK, np.float32)
    d = np.arange(DK)
    sign[(d // 32) % 2 == 0] = -1.0
    cosT = np.ascontiguousarray(cos.T)
    sinT = np.ascontiguousarray((sin * sign[None, :]).T)
    return c, cosT, sinT


_PERM = None


def _perm():
    d = np.arange(DK)
    a = d // 64
    pr = (d // 32) % 2
    f = d % 32
    return a * 64 + (1 - pr) * 32 + f


def _kc_layout(w):
    n = w.shape[1]
    return np.ascontiguousarray(w.reshape(KC, P, n).transpose(1, 0, 2))


def prepare_shared(inp, NL):
    L = inp["w_in"].shape[0]
    w_in = inp["w_in"]
    perm = _perm()
    cst, cosT, sinT = _consts(NL)
    sh = {"cst": cst, "cosT": cosT, "sinT": sinT}
    sh["wada"] = np.stack([_kc_layout(inp["w_ada"][l]) for l in range(L)])
    ba = inp["b_ada"]
    sh["badac"] = np.ascontiguousarray(ba[:, :2048].reshape(L, 16, P).transpose(0, 2, 1))
    sh["bgate"] = np.ascontiguousarray(np.broadcast_to(ba[:, None, 2048:], (L, P, D)))
    sh["ngain"] = np.ascontiguousarray(inp["norm_gain"].reshape(L, KC, P).transpose(0, 2, 1))
    sh["rgain"] = np.ascontiguousarray(np.broadcast_to(inp["ret_norm_gain"][:, None, :], (L, P, D)))
    sh["ggain"] = np.ascontiguousarray(np.broadcast_to(inp["gla_norm_gain"][:, None, :], (L, P, D)))
    sh["fgain"] = np.ascontiguousarray(np.broadcast_to(inp["final_norm_gain"][None, :], (P, D)))
    sh["rdec"] = np.ascontiguousarray(np.broadcast_to(inp["ret_decay"].reshape(L, 1, 8), (L, P, 8)))
    wup = np.zeros((L, 17, NH, 2, P), np.float32)
    for dr in range(2):
        wup[:, :16, :, dr, :] = inp["gla_w_up"][:, dr].reshape(L, 16, NH, P)
        wup[:, 16, :, dr, :] = inp["gla_b_up"][:, dr].reshape(L, NH, P)
    sh["wup"] = wup.reshape(L, 17, 1024)
    wret = np.zeros((L, NH, P, KC, 1024), np.float32)
    wgla = np.zeros((L, NH, P, KC, 768), np.float32)
    for l in range(L):
        for h in range(NH):
            q = w_in[l][:, h * 128:(h + 1) * 128]
            k = w_in[l][:, 512 + h * 128:512 + (h + 1) * 128]
            v = w_in[l][:, 1024 + h * 256:1024 + (h + 1) * 256]
            g = w_in[l][:, 2048 + h * 256:2048 + (h + 1) * 256]
            wret[l, h] = _kc_layout(np.concatenate([q, q[:, perm], k, k[:, perm], v, g], axis=1))
            q = w_in[l][:, 3072 + h * 128:3072 + (h + 1) * 128]
            k = w_in[l][:, 3584 + h * 128:3584 + (h + 1) * 128]
            v = w_in[l][:, 4096 + h * 256:4096 + (h + 1) * 256]
            g = w_in[l][:, 5120 + h * 256:5120 + (h + 1) * 256]
            wgla[l, h] = _kc_layout(np.concatenate([q, k, v, g], axis=1))
    sh["wret"] = wret
    sh["wgla"] = wgla
    sh["wglr"] = np.stack([_kc_layout(w_in[l][:, 6144:6160]) for l in range(L)])
    sh["wm"] = np.stack([np.stack([_kc_layout(w_in[l][:, 6160:7184]), _kc_layout(w_in[l][:, 7184:8208])]) for l in range(L)])
    sh["wbr"] = np.stack([np.stack([_kc_layout(inp["w_branch_ret"][l]), _kc_layout(inp["w_branch_gla"][l])]) for l in range(L)])
    sh["wout"] = np.stack([_kc_layout(inp["w_out"][l]) for l in range(L)])
    return sh


def run(inp, n_cores=None):
    inp = {k: np.asarray(v, dtype=np.float32) for k, v in inp.items()}
    B, SEQ, _ = inp["x"].shape
    CTX = inp["ctx"].shape[1]
    L = inp["w_in"].shape[0]
    NL = SEQ // P
    NCT = CTX // P
    nc = build_program(NCT, NL, L)
    sh = prepare_shared(inp, NL)
    in_maps = []
    for b in range(B):
        m = dict(sh)
        m["x"] = np.ascontiguousarray(inp["x"][b])
        m["ctx"] = np.ascontiguousarray(inp["ctx"][b])
        cc = np.stack([inp["c"][b].reshape(KC, P).T, inp["c_ctx"].reshape(KC, P).T], axis=2)
        m["ccol"] = np.ascontiguousarray(cc.reshape(P, 16))
        in_maps.append(m)
    res = run_bass_kernel_spmd(nc, in_maps, core_ids=list(range(B)))
    return np.stack([r["y"] for r in res.results], axis=0).astype(np.float32)


def kernel(**inputs):
    return run(inputs)
```

```python
import math
import numpy as np
import concourse.bass as bass
import concourse.mybir as mybir
from concourse.bass_utils import run_bass_kernel_spmd

F32 = mybir.dt.float32
BF16 = mybir.dt.bfloat16
AF = mybir.ActivationFunctionType
ALU = mybir.AluOpType

P = 128
D = 1024
KC = 8
NH = 4
DK = 128
DV = 256
EPS = 1e-6
QSCALE = DK ** -0.5
LNQ = math.log(QSCALE)
SAME_ENGINE_SYNC = True
import os
DBG_STOP = os.environ.get("KDBG", "")
DBG2 = int(os.environ.get("KDBG2", "0"))


class Res:
    __slots__ = ("w", "r")

    def __init__(self):
        self.w = None
        self.r = {}


class V:
    __slots__ = ("ap", "res")

    def __init__(self, ap, res):
        self.ap = ap
        self.res = tuple(res)

    def __getitem__(self, idx):
        return V(self.ap[idx], self.res)


def _resl(vs):
    out = []
    for v in vs:
        if isinstance(v, V):
            for r in v.res:
                if r not in out:
                    out.append(r)
    return out


class Sched:
    ENG = ("pe", "dve", "act", "pool", "sp")

    def __init__(self, nc, ndma=40):
        self.nc = nc
        self.e = {"pe": nc.tensor, "dve": nc.vector, "act": nc.scalar, "pool": nc.gpsimd, "sp": nc.sync}
        self.sem = {k: nc.alloc_semaphore("s_" + k) for k in self.ENG}
        self.cnt = {k: 0 for k in self.ENG}
        self.dsem = [nc.alloc_semaphore("d%d" % i) for i in range(ndma)]
        self.dcnt = [0] * ndma
        self.dnext = 0
        self.seen = {k: {} for k in self.ENG}
        self.ninstr = 0

    def _wait(self, eng, tok, raw=True):
        key, val = tok
        if key == eng and (eng == "pe" or not SAME_ENGINE_SYNC or not raw):
            return
        s = self.seen[eng]
        if s.get(key, 0) >= val:
            return
        s[key] = val
        sem = self.sem[key] if isinstance(key, str) else self.dsem[key]
        self.e[eng].wait_ge(sem, val)

    def deps(self, eng, reads, writes):
        for r in reads:
            if r.w is not None:
                self._wait(eng, r.w)
        for r in writes:
            if r.w is not None:
                self._wait(eng, r.w, raw=False)
            for k, v in r.r.items():
                self._wait(eng, (k, v), raw=False)

    def mark(self, tok, reads, writes):
        k, v = tok
        for r in reads:
            if r.r.get(k, 0) < v:
                r.r[k] = v
        for r in writes:
            r.w = tok
            r.r = {}

    def op(self, eng, ins, reads, writes):
        reads = _resl(reads)
        writes = _resl(writes)
        self.deps(eng, reads, writes)
        i = ins()
        self.cnt[eng] += 1
        i.then_inc(self.sem[eng], 1)
        self.mark((eng, self.cnt[eng]), reads, writes)
        self.ninstr += 1
        return i

    def dma(self, eng, out, in_, **kw):
        reads = _resl([in_])
        writes = _resl([out])
        k = self.dnext
        self.dnext = (self.dnext + 1) % len(self.dsem)
        if self.dcnt[k] > 0:
            self._wait(eng, (k, self.dcnt[k]))
        self.deps(eng, reads, writes)
        i = self.e[eng].dma_start(out=out.ap, in_=in_.ap, **kw)
        self.dcnt[k] += 16
        i.then_inc(self.dsem[k], 16)
        self.mark((k, self.dcnt[k]), reads, writes)
        self.ninstr += 1
        return i

    def barrier(self):
        for eng in self.ENG:
            for o in self.ENG:
                if o != eng and self.cnt[o] > 0:
                    self._wait(eng, (o, self.cnt[o]))
            for k in range(len(self.dsem)):
                if self.dcnt[k] > 0:
                    self._wait(eng, (k, self.dcnt[k]))

    def mm(self, out, pairs, extra_reads=()):
        nc = self.nc
        reads = []
        for a, b in pairs:
            reads += [a, b]
        reads = _resl(list(reads) + list(extra_reads))
        writes = _resl([out])
        self.deps("pe", reads, writes)
        n = len(pairs)
        for i, (a, b) in enumerate(pairs):
            ins = nc.tensor.matmul(out.ap, lhsT=a.ap, rhs=b.ap, start=(i == 0), stop=(i == n - 1))
            self.ninstr += 1
        self.cnt["pe"] += 1
        ins.then_inc(self.sem["pe"], 1)
        self.mark(("pe", self.cnt["pe"]), reads, writes)

    def transposes(self, items, ident):
        nc = self.nc
        reads = _resl([b for _, b in items] + [ident])
        writes = _resl([a for a, _ in items])
        self.deps("pe", reads, writes)
        for a, b in items:
            ins = nc.tensor.transpose(a.ap, b.ap, ident.ap)
            self.ninstr += 1
        self.cnt["pe"] += 1
        ins.then_inc(self.sem["pe"], 1)
        self.mark(("pe", self.cnt["pe"]), reads, writes)

    def act(self, out, in_, func, scale=1.0, bias=0.0, accum=None):
        nc = self.nc
        reads = [in_, scale, bias]
        writes = [out, accum]
        sc = scale.ap if isinstance(scale, V) else float(scale)
        bi = bias.ap if isinstance(bias, V) else float(bias)
        kw = {}
        if accum is not None:
            kw["accum_out"] = accum.ap
        return self.op("act", lambda: nc.scalar.activation(out=out.ap, in_=in_.ap, func=func, bias=bi, scale=sc, **kw),
                       reads, writes)

    def tt(self, eng, out, in0, in1, op):
        e = self.e[eng]
        return self.op(eng, lambda: e.tensor_tensor(out=out.ap, in0=in0.ap, in1=in1.ap, op=op), [in0, in1], [out])

    def ts(self, eng, out, in0, s1, op0, s2=None, op1=None):
        e = self.e[eng]
        a1 = s1.ap if isinstance(s1, V) else float(s1)
        a2 = None if s2 is None else (s2.ap if isinstance(s2, V) else float(s2))
        kw = {}
        if op1 is not None:
            kw["op1"] = op1
        return self.op(eng, lambda: e.tensor_scalar(out=out.ap, in0=in0.ap, scalar1=a1, scalar2=a2, op0=op0, **kw),
                       [in0, s1, s2], [out])

    def stt(self, out, in0, scalar, in1, op0, op1):
        nc = self.nc
        sc = scalar.ap if isinstance(scalar, V) else float(scalar)
        return self.op("dve", lambda: nc.vector.scalar_tensor_tensor(out=out.ap, in0=in0.ap, scalar=sc, in1=in1.ap,
                                                                     op0=op0, op1=op1), [in0, scalar, in1], [out])

    def copy(self, eng, out, in_):
        if eng == "act":
            return self.act(out, in_, AF.Copy)
        e = self.e[eng]
        return self.op(eng, lambda: e.tensor_copy(out=out.ap, in_=in_.ap), [in_], [out])

    def memset(self, eng, out, val):
        e = self.e[eng]
        return self.op(eng, lambda: e.memset(out.ap, val), [], [out])


class Arena:
    def __init__(self, nc, nbytes):
        self.words = nbytes // 4
        self.t = nc.alloc_sbuf_tensor("arena", [P, self.words], F32).ap()
        self.off = 0
        self.peak = 0

    def alloc(self, shape, dt=F32, res=None):
        parts = shape[0]
        n = 1
        for s in shape[1:]:
            n *= s
        esz = 4 if dt == F32 else 2
        nw = (n * esz + 3) // 4
        nw = (nw + 7) // 8 * 8
        assert self.off + nw <= self.words, "arena overflow: need %d have %d" % (self.off + nw, self.words)
        ap = self.t[0:parts, self.off:self.off + nw]
        self.off += nw
        self.peak = max(self.peak, self.off)
        if dt != F32:
            ap = ap.bitcast(dt)
        ap = ap[:, 0:n]
        if len(shape) == 3:
            ap = ap.rearrange("p (a b) -> p a b", a=shape[1])
        return V(ap, [res if res is not None else Res()])


def build_program(NCT, NL, L):
    T = NCT + NL
    NT = T * P
    nc = bass.Bass("TRN2", target_bir_lowering=False)
    S = Sched(nc)

    def din(name, shape):
        return V(nc.dram_tensor(name, list(shape), F32, kind="ExternalInput").ap(), [])

    x_d = din("x", [NL * P, D])
    ctx_d = din("ctx", [NCT * P, D])
    ccol_d = din("ccol", [P, 16])
    wada_d = din("wada", [L, P, KC, 3072])
    badac_d = din("badac", [L, P, 16])
    bgate_d = din("bgate", [L, P, D])
    ngain_d = din("ngain", [L, P, KC])
    rgain_d = din("rgain", [L, P, D])
    ggain_d = din("ggain", [L, P, D])
    fgain_d = din("fgain", [P, D])
    rdec_d = din("rdec", [L, P, 8])
    wup_d = din("wup", [L, 17, 1024])
    wret_d = din("wret", [L, NH, P, KC, 1024])
    wgla_d = din("wgla", [L, NH, P, KC, 768])
    wglr_d = din("wglr", [L, P, KC, 16])
    wm_d = din("wm", [L, 2, P, KC, 1024])
    wbr_d = din("wbr", [L, 2, P, KC, 1024])
    wout_d = din("wout", [L, P, KC, 1024])
    cst_d = din("cst", [14, P, P])
    cos_d = din("cosT", [P, NL * P])
    sin_d = din("sinT", [P, NL * P])
    y_d = V(nc.dram_tensor("y", [NL * P, D], F32, kind="ExternalOutput").ap(), [])
    y_r = [Res() for _ in range(NL)]

    def dscr(name, shape, dt):
        return nc.dram_tensor(name, list(shape), dt, kind="Internal").ap()

    hlat_s = [dscr("hlat%d" % l, [NL * P, D], F32) for l in range(L - 1)]
    hctx_s = [dscr("hctx%d" % l, [NCT * P, D], F32) for l in range(L - 1)]
    oret_s = [dscr("oret%d" % l, [NT, D], BF16) for l in range(L)]
    ogla_s = [dscr("ogla%d" % l, [NT, D], BF16) for l in range(L)]
    m1_s = [dscr("m1_%d" % l, [NT, D], F32) for l in range(L)]
    glr_s = [dscr("glr_%d" % l, [17, NT], BF16) for l in range(L)]
    hlat_r = [[Res() for _ in range(NL)] for _ in range(L - 1)]
    hctx_r = [[Res() for _ in range(NCT)] for _ in range(L - 1)]
    oret_r = [[Res() for _ in range(T)] for _ in range(L)]
    ogla_r = [[Res() for _ in range(T)] for _ in range(L)]
    m1_r = [[Res() for _ in range(T)] for _ in range(L)]

    def rows(ap, t, res):
        return V(ap[t * P:(t + 1) * P, :], [res])

    psum_all = nc.alloc_psum_tensor("psum_all", [P, 8 * 512], F32).ap()
    pb = [V(psum_all[:, i * 512:(i + 1) * 512], [Res()]) for i in range(8)]
    pstate = {"i": 0}

    def bank2():
        m_ = pstate.get("mod", 8)
        if pstate["i"] % 2 == 1:
            pstate["i"] = (pstate["i"] + 1) % m_
        if pstate["i"] >= m_:
            pstate["i"] = 0
        i0 = pstate["i"]
        pstate["i"] = (pstate["i"] + 2) % m_
        ap = psum_all[:, i0 * 512:(i0 + 2) * 512].rearrange("p (a b) -> p a b", a=2)
        return V(ap, [pb[i0].res[0], pb[i0 + 1].res[0]])

    def bank():
        m_ = pstate.get("mod", 8)
        if pstate["i"] >= m_:
            pstate["i"] = 0
        b = pb[pstate["i"]]
        pstate["i"] = (pstate["i"] + 1) % m_
        return b

    def bank_bf(b):
        return V(b.ap.bitcast(BF16), b.res)

    A = Arena(nc, 212800)
    uT = A.alloc([P, KC, NT], BF16)
    uT_r = [Res() for _ in range(T)]

    def uT_v(kc, t0, t1):
        return V(uT.ap[:, kc, t0 * P:t1 * P], uT_r[t0:t1])

    cst = A.alloc([P, 13, P], F32)
    ident_bf = A.alloc([P, P], BF16)
    ones_bf = A.alloc([P, P], BF16)
    C_ID, C_MF, C_MB, C_DPOS, C_DNEG, C_IPOS, C_INEG, C_TRIF, C_TRIB, C_REVF, C_REVB, C_ONES, C_COLC = range(13)

    def cs(i):
        return cst[:, i, :]

    silu_col = A.alloc([P, 16], F32)
    modc = A.alloc([P, 16, 2], F32)
    Amod = A.alloc([P, KC, 2], F32)
    small = A.alloc([P, 64], F32)
    LG = A.alloc([P, 8], F32)
    wcol = A.alloc([P, 8], F32)
    g128 = A.alloc([P, 8], F32)
    pers_mark = A.off

    S.dma("sp", cst, V(cst_d.ap[0:13].rearrange("c p f -> p c f"), []))
    S.dma("pool", ident_bf, cst_d[C_ID])
    S.dma("pool", ones_bf, cst_d[C_ONES])
    pm_bf = A.alloc([P, P], BF16)
    S.dma("pool", pm_bf, cst_d[13])
    S.dma("sp", silu_col, ccol_d)
    S.act(silu_col, silu_col, AF.Silu)

    def rstd_from(out, in_, scale, eng_tmp):
        S.act(eng_tmp, in_, AF.Ln, scale=scale, bias=eps_ap)
        S.act(out, eng_tmp, AF.Exp, scale=-0.5)

    eps_t = A.alloc([P, 1], F32)
    S.memset("dve", eps_t, EPS)
    one_t = A.alloc([P, 1], F32)
    S.memset("dve", one_t, 1.0)
    lnq_t = A.alloc([P, 1], F32)
    S.memset("dve", lnq_t, LNQ)
    eps_ap = eps_t
    pers_mark = A.off

    for l in range(L):
        last = (l == L - 1)
        ctx_out = not last
        out_tiles = list(range(T)) if ctx_out else list(range(NCT, T))
        A.off = pers_mark
        S.barrier()
        badac = A.alloc([P, 16], F32)
        ngain = A.alloc([P, KC], F32)
        S.dma("sp", badac, wrapd(badac_d, l))
        S.dma("sp", ngain, wrapd(ngain_d, l))
        wblk = [A.alloc([P, KC, 512], F32) for _ in range(2)]
        pmod = bank()
        for blk in range(4):
            wb_ = wblk[blk % 2]
            S.dma("sp", wb_, V(wada_d.ap[l, :, :, blk * 512:(blk + 1) * 512], []))
            for j in range(4):
                jj = blk * 4 + j
                S.mm(pmod[:, jj * 2:jj * 2 + 2],
                     [(wb_[:, kc, j * P:(j + 1) * P], silu_col[:, kc * 2:kc * 2 + 2]) for kc in range(KC)])
        pm3 = V(pmod.ap[:, 0:32].rearrange("p (a b) -> p a b", b=2), pmod.res)
        for s in range(2):
            S.tt("dve", modc[:, :, s], pm3[:, :, s], badac, ALU.add)
            S.stt(Amod[:, :, s], modc[:, 8:16, s], 1.0, ngain, ALU.add, ALU.mult)
        S.barrier()
        A.off = pers_mark
        hts = [A.alloc([P, D], F32) for _ in range(3)]
        xns = [A.alloc([P, D], BF16) for _ in range(2)]
        junk = A.alloc([P, D], BF16)
        sst = [A.alloc([P, 4], F32) for _ in range(2)]
        def pb_A(k, t):
            if l == 0:
                src = V(ctx_d.ap[t * P:(t + 1) * P, :], []) if t < NCT else V(x_d.ap[(t - NCT) * P:(t - NCT + 1) * P, :], [])
            else:
                src = rows(hctx_s[l - 1], t, hctx_r[l - 1][t]) if t < NCT else rows(hlat_s[l - 1], t - NCT, hlat_r[l - 1][t - NCT])
            S.dma("sp", hts[k % 3], src)

        def pb_B(k, t):
            ht = hts[k % 3]
            xn = xns[k % 2]
            ss = sst[k % 2]
            S.act(junk, ht, AF.Square, accum=ss[:, 0:1])
            S.act(ss[:, 1:2], ss[:, 0:1], AF.Ln, scale=1.0 / D, bias=eps_t)
            S.act(ss[:, 2:3], ss[:, 1:2], AF.Exp, scale=-0.5)
            S.act(xn, ht, AF.Copy, scale=ss[:, 2:3])

        def pb_C(k, t):
            s_ = 1 if t < NCT else 0
            xn = xns[k % 2]
            pt = bank_bf(bank())
            S.transposes([(pt[:, kc * P:(kc + 1) * P], xn[:, kc * P:(kc + 1) * P]) for kc in range(KC)], ident_bf)
            for kc in range(KC):
                o = V(uT.ap[:, kc, t * P:(t + 1) * P], [uT_r[t]])
                S.ts("dve", o, pt[:, kc * P:(kc + 1) * P], Amod[:, kc, s_:s_ + 1], ALU.mult, modc[:, kc, s_:s_ + 1], ALU.add)

        pipeline([pb_A, pb_B, pb_C], list(range(T)))

        groups = [(0, NCT)] if NCT > 0 else []
        g = NCT
        while g < T:
            groups.append((g, min(g + 4, T)))
            g += 4
        bwd_order = list(range(NCT - 1, -1, -1)) + list(range(T - 1, NCT - 1, -1))
        assert NCT % 2 == 0 and NL % 2 == 0
        bwd_pairs = list(range(NCT - 2, -1, -2)) + list(range(T - 2, NCT - 1, -2))

        for br in range(2):
            S.barrier()
            A.off = pers_mark
            gain_src = rgain_d if br == 0 else ggain_d
            if br == 0:
                rdec = A.alloc([P, 8], F32)
                S.dma("sp", rdec, wrapd(rdec_d, l))
                Dtot1 = [A.alloc([P, P], F32) for _ in range(NH)]
                Gf1 = [A.alloc([P, P], F32) for _ in range(NH)]
                Gb1 = [A.alloc([P, P], F32) for _ in range(NH)]
                DtP = A.alloc([P, 2 * P], F32)
                GfP = A.alloc([P, 2 * P], F32)
                GbP = A.alloc([P, 2 * P], F32)
                S.act(LG, rdec, AF.Exp)
                S.act(LG, LG, AF.Ln, scale=-1.0, bias=one_t)
                tmpa = A.alloc([P, P], F32)
                tmpb = A.alloc([P, P], F32)
                for h in range(NH):
                    S.act(tmpa, cs(C_DPOS), AF.Exp, scale=LG[:, h:h + 1], bias=lnq_t)
                    S.act(tmpb, cs(C_DNEG), AF.Exp, scale=LG[:, 4 + h:5 + h], bias=lnq_t)
                    S.tt("dve", tmpa, tmpa, cs(C_MF), ALU.mult)
                    S.tt("dve", tmpb, tmpb, cs(C_MB), ALU.mult)
                    S.tt("dve", Dtot1[h], tmpa, tmpb, ALU.add)
                    S.act(Gf1[h], cs(C_IPOS), AF.Exp, scale=LG[:, h:h + 1], bias=lnq_t)
                    S.act(Gb1[h], cs(C_INEG), AF.Exp, scale=LG[:, 4 + h:5 + h], bias=lnq_t)
                    S.act(wcol[:, h:h + 1], cst[:, C_COLC, 0:1], AF.Exp, scale=LG[:, h:h + 1])
                    S.act(wcol[:, 4 + h:5 + h], cst[:, C_COLC, 1:2], AF.Exp, scale=LG[:, 4 + h:5 + h])
                    S.act(g128[:, h:h + 1], cst[:, C_COLC, 2:3], AF.Exp, scale=LG[:, h:h + 1])
                    S.act(g128[:, 4 + h:5 + h], cst[:, C_COLC, 2:3], AF.Exp, scale=LG[:, 4 + h:5 + h])
            qT = A.alloc([P, NT], BF16)
            kT = A.alloc([P, NT], BF16)
            qT_r = [Res() for _ in range(T)]
            kT_r = [Res() for _ in range(T)]
            vv = A.alloc([P, T, DV], BF16)
            gs = A.alloc([P, T, DV], BF16)
            v_r = [Res() for _ in range(T)]
            gs_r = [Res() for _ in range(T)]
            sball = A.alloc([P, T, DV], BF16)
            sb_r = [Res() for _ in range(T)]
            ktok = A.alloc([P, T, DK], BF16)
            ktok_r = [Res() for _ in range(T)]
            if br == 0:
                NCOL = 1024
                wsrc = wret_d
            else:
                NCOL = 768
                wsrc = wgla_d
                glr_r = [Res() for _ in range(T)]
                glrg = [A.alloc([17, 512], BF16) for _ in range(1)] * 2
                glrt = [A.alloc([17, 2 * P], BF16) for _ in range(2)] * 2
                wup_hs = [A.alloc([17, 2 * P], BF16) for _ in range(2)]
                wglr = A.alloc([P, KC, 16], BF16)
                S.dma("pool", wglr, wrapd(wglr_d, l))
                S.memset("dve", glrg[0], 1.0)
                for gi_, (t0, t1) in enumerate(groups):
                    n = (t1 - t0) * P
                    pg = bank()
                    S.mm(pg[0:16, 0:n], [(wglr[:, kc, :], uT_v(kc, t0, t1)) for kc in range(KC)])
                    gg_ = glrg[gi_ % 2]
                    S.act(gg_[0:16, 0:n], pg[0:16, 0:n], AF.Copy)
                    S.dma("sp", V(glr_s[l][:, t0 * P:t1 * P], glr_r[t0:t1]), gg_[:, 0:n])
                glr_i = {"i": 0}

                def glr_load(t):
                    b_ = glrt[glr_i["i"] % 4]
                    glr_i["i"] += 1
                    S.dma("sp", b_, V(glr_s[l][:, t * P:(t + 2) * P], glr_r[t:t + 2]))
                    return b_
            wsb = A.alloc([P, KC, NCOL], BF16)
            Sf2 = [A.alloc([P, DV], F32) for _ in range(2)]
            Sb2 = [A.alloc([P, DV], F32) for _ in range(2)]
            Sfb2 = [A.alloc([P, DV], BF16) for _ in range(2)]
            NB = 3
            sgs = [A.alloc([P, 2, DV], F32) for _ in range(1)] * 2
            wk_kd = [A.alloc([P, 2 * P], BF16) for _ in range(NB)]
            wk_q = [[A.alloc([P, 2 * P], BF16) for _ in range(NB)] for _ in range(4 if br == 1 else 2)]
            wk_sm = [A.alloc([P, (4 if br == 1 else 2) * P], BF16) for _ in range(NB)]
            af_t = [A.alloc([P, 4], F32) for _ in range(NB)]
            if br == 0:
                costs = [A.alloc([P, 512], F32) for _ in range(1)] * 2
                sints = [A.alloc([P, 512], F32) for _ in range(1)] * 2
                rawb = A.alloc([P, 512], BF16)
                t1s = [A.alloc([P, 512], F32) for _ in range(1)]
                t2s = [A.alloc([P, 512], F32) for _ in range(1)]
            else:
                nls_t = [A.alloc([P, 4 * P], F32) for _ in range(2)] + [None]
                nls_t[2] = None
                ex_t = [A.alloc([P, 4 * P], F32) for _ in range(1)] * 2
                E1_t = [A.alloc([P, 4 * P], F32) for _ in range(1)] * 2
                E2_t = [A.alloc([P, 4 * P], F32) for _ in range(1)] * 2
                e3_t = [A.alloc([P, 2 * P], F32) for _ in range(1)] * 2
                mask4 = A.alloc([P, 4 * P], BF16)
                for i_ in range(4):
                    S.copy("dve", mask4[:, i_ * P:(i_ + 1) * P], cs(C_MF if i_ < 2 else C_MB))
            on_t = [A.alloc([P, 2, DV], F32) for _ in range(1)] * 2
            ot_t = [A.alloc([P, 2, DV], BF16) for _ in range(2)] + [None]
            ot_t[2] = ot_t[0]
            st_t = [A.alloc([P, 32], F32) for _ in range(3)]
            gh2s = [A.alloc([P, 2, DV], F32) for _ in range(1)] * 2

            def flat(v_):
                return V(v_.ap.rearrange("p a b -> p (a b)"), v_.res)

            def qT_v(t0, t1):
                return V(qT.ap[:, t0 * P:t1 * P], qT_r[t0:t1])

            def kT_v(t0, t1):
                return V(kT.ap[:, t0 * P:t1 * P], kT_r[t0:t1])

            def tl(arr, rl, t):
                return V(arr.ap[:, t, :], [rl[t]])

            def load_head_w(hh):
                for kc in range(KC):
                    S.dma("pool", wsb[:, kc, :], V(wsrc.ap[l, hh, :, kc, :], []))
                for n_ in range(2):
                    S.dma("sp", gh2s[hh % 2][:, n_, :], V(gain_src.ap[l, :, hh * DV:(hh + 1) * DV], []))
                if br == 1:
                    S.dma("pool", wup_hs[hh % 2], V(wup_d.ap[l, :, hh * 2 * P:(hh * 2 + 2) * P], []))

            load_head_w(0)
            if DBG_STOP == "pre%d" % br:
                S.barrier()
                return nc
            for h in range(NH):
                gh2 = gh2s[h % 2]
                if br == 1:
                    wup_h = wup_hs[h % 2]
                if br == 0:
                    for n_ in range(2):
                        S.copy("dve", DtP[:, n_ * P:(n_ + 1) * P], Dtot1[h])
                        S.copy("dve", GfP[:, n_ * P:(n_ + 1) * P], Gf1[h])
                        S.copy("dve", GbP[:, n_ * P:(n_ + 1) * P], Gb1[h])

                def proj_fm(gi, grp):
                    t0, t1 = grp
                    n = (t1 - t0) * P
                    is_ctx = t0 < NCT
                    if br == 0 and not is_ctx:
                        ct = costs[gi % 2]
                        st = sints[gi % 2]
                        lt0 = (t0 - NCT) * P
                        S.dma("sp", ct[:, 0:n], V(cos_d.ap[:, lt0:lt0 + n], []))
                        S.dma("sp", st[:, 0:n], V(sin_d.ap[:, lt0:lt0 + n], []))
                        for (blk, dstv) in ((0, qT_v(t0, t1)), (2, kT_v(t0, t1))):
                            pa = bank()
                            pbk = bank()
                            S.mm(pa[:, 0:n], [(wsb[:, kc, blk * P:(blk + 1) * P], uT_v(kc, t0, t1)) for kc in range(KC)])
                            S.copy("dve", rawb[:, 0:n], pa[:, 0:n])
                            S.mm(pbk[:, 0:n], [(pm_bf, rawb[:, 0:n])])
                            ta = t1s[0]
                            tb = t2s[0]
                            S.tt("dve", ta[:, 0:n], pa[:, 0:n], ct[:, 0:n], ALU.mult)
                            S.tt("dve", tb[:, 0:n], pbk[:, 0:n], st[:, 0:n], ALU.mult)
                            S.tt("pool", dstv, ta[:, 0:n], tb[:, 0:n], ALU.add)
                    else:
                        qb, kb = (0, 2) if br == 0 else (0, 1)
                        pa = bank()
                        S.mm(pa[:, 0:n], [(wsb[:, kc, qb * P:(qb + 1) * P], uT_v(kc, t0, t1)) for kc in range(KC)])
                        S.act(qT_v(t0, t1), pa[:, 0:n], AF.Copy, scale=(1.0 if br == 0 else QSCALE))
                        pbk = bank()
                        S.mm(pbk[:, 0:n], [(wsb[:, kc, kb * P:(kb + 1) * P], uT_v(kc, t0, t1)) for kc in range(KC)])
                        S.copy("dve", kT_v(t0, t1), pbk[:, 0:n])

                def proj_tm_pair(tlo):
                    vb = 512 if br == 0 else 256
                    pT2 = bank2()
                    for n_ in range(2):
                        t = tlo + n_
                        S.mm(pT2[:, n_, :], [(uT_v(kc, t, t + 1), wsb[:, kc, vb:vb + 512]) for kc in range(KC)])
                    gpair = V(gs.ap[:, tlo:tlo + 2, :].rearrange("p a b -> p (a b)"), gs_r[tlo:tlo + 2])
                    sg = sgs[(tlo // 2) % 2]
                    for n_ in range(2):
                        if not (DBG2 & 1):
                            S.copy("act", tl(vv, v_r, tlo + n_), pT2[:, n_, 0:DV])
                        if not (DBG2 & 2):
                            S.act(sg[:, n_, :], pT2[:, n_, DV:2 * DV], AF.Silu)
                    if not (DBG2 & 8):
                        S.tt("pool", gpair, flat(sg), flat(gh2), ALU.mult)
                    if not (DBG2 & 4):
                        ptk = bank_bf(bank())
                        S.transposes([(ptk[:, n_ * P:(n_ + 1) * P], kT_v(tlo + n_, tlo + n_ + 1)) for n_ in range(2)], ident_bf)
                        kpair = V(ktok.ap[:, tlo:tlo + 2, :].rearrange("p a b -> p (a b)"), ktok_r[tlo:tlo + 2])
                        S.copy("act", kpair, ptk[:, 0:2 * P])

                S.memset("dve", Sb2[0], 0.0)
                bst = {}
                first_of_group = {}
                for gi_, grp in enumerate(groups):
                    t0_, t1_ = grp
                    prs_ = [p_ for p_ in bwd_pairs if t0_ <= p_ < t1_]
                    first_of_group[prs_[0]] = (gi_, grp)

                def st_fm(k, tlo):
                    if tlo in first_of_group:
                        gi_, grp = first_of_group[tlo]
                        proj_fm(gi_, grp)

                def st_tm(k, tlo):
                    bst.setdefault(k, {})
                    proj_tm_pair(tlo)

                def st_tm_L(k, tlo):
                    proj_tm_pair(tlo)
                    bwd_L(k, tlo)

                def ktok_pair(tlo):
                    return V(ktok.ap[:, tlo:tlo + 2, :].rearrange("p a b -> p (a b)"), ktok_r[tlo:tlo + 2])

                def bwd_L(k, tlo):
                    c = bst.setdefault(k, {})
                    if br == 1:
                        c["gl"] = glr_load(tlo)

                def bwd_A(k, tlo):
                    c = bst[k]
                    if br == 1:
                        gl = c["gl"]
                        plg = bank()
                        for n_ in range(2):
                            S.mm(plg[:, n_ * P:(n_ + 1) * P], [(gl[:, n_ * P:(n_ + 1) * P], wup_h[0:17, P:2 * P])])
                        ex = ex_t[k % 2]
                        nl = nls_t[k % 2]
                        S.act(ex[:, 0:2 * P], plg[:, 0:2 * P], AF.Exp, scale=-1.0)
                        S.act(nl[:, 0:2 * P], ex[:, 0:2 * P], AF.Ln, scale=1.0, bias=one_t)
                        c["nl"] = nl

                def bwd_B(k, tlo):
                    c = bst[k]
                    kd = wk_kd[k % NB]
                    if br == 0:
                        S.act(kd, ktok_pair(tlo), AF.Copy, scale=wcol[:, 4 + h:5 + h])
                        c["a"] = [g128[:, 4 + h:5 + h], g128[:, 4 + h:5 + h]]
                    else:
                        nl = c["nl"]
                        prv = bank()
                        for n_ in range(2):
                            S.mm(prv[:, n_ * P:(n_ + 1) * P], [(cs(C_REVB), nl[:, n_ * P:(n_ + 1) * P])])
                        for n_ in range(2):
                            S.mm(prv[:, 2 * P + 2 * n_:2 * P + 2 * n_ + 2], [(nl[:, n_ * P:(n_ + 1) * P], cst[:, C_ONES, 0:2])])
                        e3 = e3_t[k % 2]
                        ab = af_t[k % NB]
                        S.act(e3, prv[:, 0:2 * P], AF.Exp, scale=-1.0 / 16.0)
                        S.act(ab[:, 0:4], prv[:, 2 * P:2 * P + 4], AF.Exp, scale=-1.0 / 16.0)
                        S.tt("pool", kd, ktok_pair(tlo), e3, ALU.mult)
                        c["a"] = [ab[:, 0:1], ab[:, 2:3]]
                    c["kd"] = kd

                def bwd_D(k, tlo):
                    c = bst.pop(k)
                    for j_, n_ in enumerate((1, 0)):
                        t = tlo + n_
                        kk = 2 * k + j_
                        need_sb = (t >= NCT) or ctx_out
                        Sb_cur = Sb2[kk % 2]
                        Sb_nxt = Sb2[(kk + 1) % 2]
                        pkv = bank()
                        S.mm(pkv[:, 0:DV], [(c["kd"][:, n_ * P:(n_ + 1) * P], tl(vv, v_r, t))])
                        S.stt(Sb_nxt, Sb_cur, c["a"][n_], pkv[:, 0:DV], ALU.mult, ALU.add)
                        if need_sb:
                            S.copy("pool", tl(sball, sb_r, t), Sb_cur)

                if DBG_STOP == "proj%d" % br:
                    pipeline([st_fm, None, st_tm], bwd_pairs)
                    S.barrier()
                    return nc
                pipeline([st_fm, None, st_tm if br == 0 else st_tm_L, bwd_A, bwd_B, bwd_D], bwd_pairs)
                if DBG_STOP in ("bwd%d" % br, "bwd%d:%d:%d" % (br, l, h)):
                    S.barrier()
                    return nc
                if h + 1 < NH:
                    load_head_w(h + 1)

                S.memset("dve", Sf2[0], 0.0)
                S.memset("pool", Sfb2[0], 0.0)
                odst, odst_r = (oret_s[l], oret_r[l]) if br == 0 else (ogla_s[l], ogla_r[l])
                fst = {}

                def fwd_L(k, tlo):
                    c = fst.setdefault(k, {})
                    if br == 1:
                        c["gl"] = glr_load(tlo)

                def fwd_A(k, tlo):
                    c = fst[k]
                    if br == 1:
                        gl = c["gl"]
                        plg = bank()
                        for n_ in range(2):
                            S.mm(plg[:, n_ * 2 * P:(n_ + 1) * 2 * P], [(gl[:, n_ * P:(n_ + 1) * P], wup_h[0:17, 0:2 * P])])
                        ex = ex_t[k % 2]
                        nl = nls_t[k % 2]
                        S.act(ex, plg, AF.Exp, scale=-1.0)
                        S.act(nl, ex, AF.Ln, scale=1.0, bias=one_t)
                        c["nl"] = nl

                def fwd_B(k, tlo):
                    c = fst[k]
                    need_out = (tlo >= NCT) or ctx_out
                    kb_ = k % NB
                    kd = wk_kd[kb_]
                    c["kd"] = kd
                    qp = qT_v(tlo, tlo + 2)
                    kp = kT_v(tlo, tlo + 2)
                    if br == 0:
                        S.act(kd, ktok_pair(tlo), AF.Copy, scale=wcol[:, h:h + 1])
                        c["a"] = [g128[:, h:h + 1], g128[:, h:h + 1]]
                        if need_out:
                            qdf = wk_q[0][kb_]
                            qdb = wk_q[1][kb_]
                            S.tt("pool", qdf, qp, GfP, ALU.mult)
                            S.tt("dve", qdb, qp, GbP, ALU.mult)
                            c["qf"], c["qb"] = qdf, qdb
                    else:
                        nl = c["nl"]
                        px = bank()
                        py_ = bank()
                        for n_ in range(2):
                            S.mm(px[:, n_ * P:(n_ + 1) * P], [(nl[:, n_ * 2 * P:n_ * 2 * P + P], cs(C_TRIF))])
                        for n_ in range(2):
                            S.mm(px[:, 2 * P + n_ * P:2 * P + (n_ + 1) * P], [(nl[:, n_ * 2 * P + P:(n_ + 1) * 2 * P], cs(C_TRIB))])
                        for n_ in range(2):
                            S.mm(py_[:, n_ * P:(n_ + 1) * P], [(cs(C_REVF), nl[:, n_ * 2 * P:n_ * 2 * P + P])])
                        E1 = E1_t[k % 2]
                        E2 = E2_t[k % 2]
                        e3f = e3_t[k % 2]
                        S.act(E1, px, AF.Exp, scale=-1.0 / 16.0)
                        S.act(e3f, py_[:, 0:2 * P], AF.Exp, scale=-1.0 / 16.0)
                        af = af_t[kb_]
                        S.act(af[:, 0:1], px[:, P - 1:P], AF.Exp, scale=-1.0 / 16.0)
                        S.act(af[:, 2:3], px[:, 2 * P - 1:2 * P], AF.Exp, scale=-1.0 / 16.0)
                        c["a"] = [af[:, 0:1], af[:, 2:3]]
                        S.tt("pool", kd, ktok_pair(tlo), e3f, ALU.mult)
                        if need_out:
                            S.act(E2, px, AF.Exp, scale=1.0 / 16.0)
                            qgf, qgb, krf, krb = (wk_q[i][kb_] for i in range(4))
                            S.tt("dve", qgf, qp, E1[:, 0:2 * P], ALU.mult)
                            S.tt("pool", qgb, qp, E1[:, 2 * P:4 * P], ALU.mult)
                            S.tt("pool", krf, kp, E2[:, 0:2 * P], ALU.mult)
                            S.tt("pool", krb, kp, E2[:, 2 * P:4 * P], ALU.mult)
                            c["qf"], c["qb"], c["krf"], c["krb"] = qgf, qgb, krf, krb

                def fwd_C(k, tlo):
                    c = fst[k]
                    need_out = (tlo >= NCT) or ctx_out
                    kb_ = k % NB
                    pkvb = pb[4 + (k % 2)]
                    c["pkv"] = pkvb
                    for n_ in range(2):
                        t = tlo + n_
                        if t < T - 1:
                            S.mm(pkvb[:, n_ * DV:(n_ + 1) * DV], [(c["kd"][:, n_ * P:(n_ + 1) * P], tl(vv, v_r, t))])
                    if not need_out:
                        return
                    ps = bank()
                    sm = wk_sm[kb_]
                    if br == 0:
                        for n_ in range(2):
                            S.mm(ps[:, n_ * P:(n_ + 1) * P], [(kT_v(tlo + n_, tlo + n_ + 1), qT_v(tlo + n_, tlo + n_ + 1))])
                        S.tt("dve", sm[:, 0:2 * P], ps[:, 0:2 * P], DtP, ALU.mult)
                    else:
                        for n_ in range(2):
                            S.mm(ps[:, n_ * P:(n_ + 1) * P], [(c["krf"][:, n_ * P:(n_ + 1) * P], c["qf"][:, n_ * P:(n_ + 1) * P])])
                        for n_ in range(2):
                            S.mm(ps[:, 2 * P + n_ * P:2 * P + (n_ + 1) * P], [(c["krb"][:, n_ * P:(n_ + 1) * P], c["qb"][:, n_ * P:(n_ + 1) * P])])
                        S.tt("dve", sm, ps, mask4, ALU.mult)
                    c["sm"] = sm

                def fwd_D(k, tlo):
                    c = fst[k]
                    need_out = (tlo >= NCT) or ctx_out
                    po = pb[6 + (k % 2)]
                    c["po"] = po
                    stt_ = st_t[k % 3]
                    c["st"] = stt_
                    pkvb = c["pkv"]
                    for n_ in range(2):
                        t = tlo + n_
                        kk = 2 * k + n_
                        ns_ = slice(n_ * P, (n_ + 1) * P)
                        if need_out:
                            sm = c["sm"]
                            prs_ = [(sm[:, ns_], tl(vv, v_r, t))]
                            if br == 1:
                                prs_.append((sm[:, 2 * P + n_ * P:2 * P + (n_ + 1) * P], tl(vv, v_r, t)))
                            prs_ += [(c["qf"][:, ns_], Sfb2[kk % 2]), (c["qb"][:, ns_], tl(sball, sb_r, t))]
                            S.mm(po[:, n_ * DV:(n_ + 1) * DV], prs_)
                        if t < T - 1:
                            S.stt(Sf2[(kk + 1) % 2], Sf2[kk % 2], c["a"][n_], pkvb[:, n_ * DV:(n_ + 1) * DV], ALU.mult, ALU.add)
                            S.copy("act" if br == 0 else "dve", Sfb2[(kk + 1) % 2], Sf2[(kk + 1) % 2])

                def fwd_E(k, tlo):
                    c = fst.pop(k)
                    need_out = (tlo >= NCT) or ctx_out
                    if not need_out:
                        return
                    stt_ = c["st"]
                    po = c["po"]
                    on = on_t[k % 2]
                    ot = ot_t[k % 2]
                    for n_ in range(2):
                        pon = po[:, n_ * DV:(n_ + 1) * DV]
                        S.op("dve", lambda: nc.vector.bn_stats(out=stt_.ap[:, n_ * 6:n_ * 6 + 6], in_=pon.ap), [po], [stt_])
                        S.op("dve", lambda: nc.vector.bn_aggr(out=stt_.ap[:, 12 + 2 * n_:14 + 2 * n_], in_=stt_.ap[:, n_ * 6:n_ * 6 + 6]), [stt_], [stt_])
                    mv = V(stt_.ap[:, 12:16].rearrange("p (a b) -> p a b", a=2), stt_.res)
                    if br == 0:
                        S.act(stt_[:, 16:18], mv[:, :, 1], AF.Ln, scale=1.0, bias=eps_t)
                        S.act(stt_[:, 18:20], stt_[:, 16:18], AF.Exp, scale=-0.5)
                        S.stt(stt_[:, 20:22], mv[:, :, 0], -1.0, stt_[:, 18:20], ALU.mult, ALU.mult)
                        for n_ in range(2):
                            S.act(on[:, n_, :], po[:, n_ * DV:(n_ + 1) * DV], AF.Identity, scale=stt_[:, 18 + n_:19 + n_], bias=stt_[:, 20 + n_:21 + n_])
                    else:
                        S.tt("dve", stt_[:, 22:24], mv[:, :, 0], mv[:, :, 0], ALU.mult)
                        S.tt("dve", stt_[:, 22:24], stt_[:, 22:24], mv[:, :, 1], ALU.add)
                        S.act(stt_[:, 16:18], stt_[:, 22:24], AF.Ln, scale=1.0, bias=eps_t)
                        S.act(stt_[:, 18:20], stt_[:, 16:18], AF.Exp, scale=-0.5)
                        for n_ in range(2):
                            S.stt(ot[:, n_, :], po[:, n_ * DV:(n_ + 1) * DV], stt_[:, 18 + n_:19 + n_],
                                  tl(gs, gs_r, tlo + n_), ALU.mult, ALU.mult)
                    if br == 0:
                        gpair = V(gs.ap[:, tlo:tlo + 2, :].rearrange("p a b -> p (a b)"), gs_r[tlo:tlo + 2])
                        S.tt("pool", flat(ot), flat(on), gpair, ALU.mult)
                    dst = V(odst[tlo * P:(tlo + 2) * P, h * DV:(h + 1) * DV].rearrange("(a p) c -> p a c", a=2), odst_r[tlo:tlo + 2])
                    S.dma("sp", dst, ot)

                pstate["mod"] = 4
                pipeline([fwd_L, fwd_A, fwd_B, fwd_C, fwd_D, fwd_E], list(range(0, T, 2)), order=[4, 5, 3, 2, 1, 0])
                pstate["mod"] = 8
                if DBG_STOP in ("fwd%d" % br, "fwd%d:%d:%d" % (br, l, h)):
                    S.barrier()
                    return nc

            S.barrier()
            A.off = pers_mark
            wbr = A.alloc([P, KC, D], BF16)
            wmm = A.alloc([P, KC, D], BF16)
            for kc in range(KC):
                S.dma("pool", wbr[:, kc, :], V(wbr_d.ap[l, br, :, kc, :], []))
                S.dma("pool", wmm[:, kc, :], V(wm_d.ap[l, br, :, kc, :], []))
            ots = [A.alloc([P, D], BF16) for _ in range(3)]
            oTs = [A.alloc([P, KC, P], BF16) for _ in range(2)]
            sgt = [A.alloc([P, 512], F32) for _ in range(2)]
            osrc, osrc_r = (oret_s[l], oret_r[l]) if br == 0 else (ogla_s[l], ogla_r[l])

            def hsrc(t):
                if l == 0:
                    return V(ctx_d.ap[t * P:(t + 1) * P, :], []) if t < NCT else V(x_d.ap[(t - NCT) * P:(t - NCT + 1) * P, :], [])
                return rows(hctx_s[l - 1], t, hctx_r[l - 1][t]) if t < NCT else rows(hlat_s[l - 1], t - NCT, hlat_r[l - 1][t - NCT])

            if br == 0:
                m1t = [A.alloc([P, D], F32) for _ in range(2)]

                def dr_L(k, t):
                    S.dma("sp", ots[k % 3], rows(osrc, t, osrc_r[t]))

                def dr_A(k, t):
                    ot = ots[k % 3]
                    pt = bank_bf(bank())
                    S.transposes([(pt[:, kc * P:(kc + 1) * P], ot[:, kc * P:(kc + 1) * P]) for kc in range(KC)], ident_bf)
                    oT = oTs[k % 2]
                    S.copy("dve", V(oT.ap.rearrange("p a b -> p (a b)"), oT.res), pt)

                def dr_B(k, t):
                    oT = oTs[k % 2]
                    mt = m1t[k % 2]
                    for half in range(2):
                        py = bank()
                        pm = bank()
                        S.mm(pm, [(uT_v(kc, t, t + 1), wmm[:, kc, half * 512:(half + 1) * 512]) for kc in range(KC)])
                        S.mm(py, [(oT[:, kc, :], wbr[:, kc, half * 512:(half + 1) * 512]) for kc in range(KC)])
                        sg = sgt[half]
                        S.act(sg, pm, AF.Sigmoid)
                        S.tt("dve", mt[:, half * 512:(half + 1) * 512], py, sg, ALU.mult)
                    S.dma("sp", rows(m1_s[l], t, m1_r[l][t]), mt)

                pipeline([dr_L, dr_A, dr_B], out_tiles)
                if DBG_STOP == "pd0:%d" % l:
                    S.barrier()
                    return nc
            else:
                wout = A.alloc([P, KC, D], BF16)
                for kc in range(KC):
                    S.dma("pool", wout[:, kc, :], V(wout_d.ap[l, :, kc, :], []))
                ns = 2 if ctx_out else 1
                gate_bc = [A.alloc([P, D], F32) for _ in range(ns)]
                gate_mark = A.off
                bg = A.alloc([P, D], F32)
                S.dma("sp", bg, wrapd(bgate_d, l))
                srep = A.alloc([P, KC, P], F32)
                wg_ = [A.alloc([P, KC, 512], F32) for _ in range(2)]
                for blk in range(2):
                    S.dma("sp", wg_[blk], V(wada_d.ap[l, :, :, 2048 + blk * 512:2048 + (blk + 1) * 512], []))
                for s in range(ns):
                    for kc in range(KC):
                        S.ts("dve", srep[:, kc, :], cs(C_ONES), silu_col[:, kc * 2 + s:kc * 2 + s + 1], ALU.mult)
                    for blk in range(2):
                        pgt = bank()
                        S.mm(pgt, [(srep[:, kc, :], wg_[blk][:, kc, :]) for kc in range(KC)])
                        S.tt("dve", gate_bc[s][:, blk * 512:(blk + 1) * 512], pgt, bg[:, blk * 512:(blk + 1) * 512], ALU.add)
                S.barrier()
                A.off = gate_mark
                if last:
                    fg = A.alloc([P, D], F32)
                    S.dma("sp", fg, fgain_d)
                m1t = [A.alloc([P, D], F32) for _ in range(3)]
                hts2x = [A.alloc([P, D], F32) for _ in range(4)]
                mgs = [A.alloc([P, D], BF16) for _ in range(2)]
                mgT = [A.alloc([P, KC, P], BF16) for _ in range(2)]
                tmpf = [A.alloc([P, 512], F32) for _ in range(2)]
                hn_t = [A.alloc([P, D], F32) for _ in range(2)]
                junk2 = A.alloc([P, D], BF16)
                sst2 = [A.alloc([P, 4], F32) for _ in range(2)]

                def dg_L(k, t):
                    S.dma("sp", ots[k % 3], rows(osrc, t, osrc_r[t]))
                    S.dma("sp", hts2x[k % 4], hsrc(t))
                    S.dma("sp", m1t[k % 3], rows(m1_s[l], t, m1_r[l][t]))

                def dg_A(k, t):
                    ot = ots[k % 3]
                    pt = bank_bf(bank())
                    S.transposes([(pt[:, kc * P:(kc + 1) * P], ot[:, kc * P:(kc + 1) * P]) for kc in range(KC)], ident_bf)
                    oT = oTs[k % 2]
                    S.copy("dve", V(oT.ap.rearrange("p a b -> p (a b)"), oT.res), pt)

                def dg_B(k, t):
                    oT = oTs[k % 2]
                    mt = m1t[k % 3]
                    mg = mgs[k % 2]
                    for half in range(2):
                        hs = slice(half * 512, (half + 1) * 512)
                        py = bank()
                        pm = bank()
                        S.mm(pm, [(uT_v(kc, t, t + 1), wmm[:, kc, hs]) for kc in range(KC)])
                        S.mm(py, [(oT[:, kc, :], wbr[:, kc, hs]) for kc in range(KC)])
                        sg = sgt[half]
                        S.act(sg, pm, AF.Sigmoid)
                        tf = tmpf[half]
                        S.tt("dve", tf, py, sg, ALU.mult)
                        S.tt("pool", mg[:, hs], tf, mt[:, hs], ALU.add)

                def dg_C(k, t):
                    mg = mgs[k % 2]
                    pt2 = bank_bf(bank())
                    S.transposes([(pt2[:, kc * P:(kc + 1) * P], mg[:, kc * P:(kc + 1) * P]) for kc in range(KC)], ident_bf)
                    mT = mgT[k % 2]
                    S.copy("act", V(mT.ap.rearrange("p a b -> p (a b)"), mT.res), pt2)

                def dg_D(k, t):
                    s = 1 if t < NCT else 0
                    mT = mgT[k % 2]
                    ht = hts2x[k % 4]
                    hn = hn_t[k % 2]
                    for half in range(2):
                        hs = slice(half * 512, (half + 1) * 512)
                        po = bank()
                        S.mm(po, [(mT[:, kc, :], wout[:, kc, hs]) for kc in range(KC)])
                        tf = tmpf[half]
                        S.tt("dve", tf, po, gate_bc[s][:, hs], ALU.mult)
                        S.tt("pool", hn[:, hs], tf, ht[:, hs], ALU.add)
                    if last:
                        ss = sst2[k % 2]
                        S.act(junk2, hn, AF.Square, accum=ss[:, 0:1])
                        S.act(ss[:, 1:2], ss[:, 0:1], AF.Ln, scale=1.0 / D, bias=eps_t)
                        S.act(ss[:, 2:3], ss[:, 1:2], AF.Exp, scale=-0.5)
                        S.act(ht, hn, AF.Copy, scale=ss[:, 2:3])
                        S.tt("dve", hn, ht, fg, ALU.mult)
                        S.dma("sp", V(y_d.ap[(t - NCT) * P:(t - NCT + 1) * P, :], [y_r[t - NCT]]), hn)
                    else:
                        if t < NCT:
                            S.dma("sp", rows(hctx_s[l], t, hctx_r[l][t]), hn)
                        else:
                            S.dma("sp", rows(hlat_s[l], t - NCT, hlat_r[l][t - NCT]), hn)

                pipeline([dg_L, dg_A, dg_B, dg_C, dg_D], out_tiles)
                if DBG_STOP == "pd1:%d" % l:
                    S.barrier()
                    return nc
    S.barrier()
    build_program.stats = (S.ninstr, dict(S.cnt), A.peak * 4)
    return nc


def pipeline(stages, items, order=None):
    n = len(items)
    ns = len(stages)
    if order is None:
        order = list(range(ns - 1, -1, -1))
    for i in range(n + ns - 1):
        for s in order:
            k = i - s
            if 0 <= k < n and stages[s] is not None:
                stages[s](k, items[k])


def wrapd(v, l):
    return V(v.ap[l], [])


def _consts(NL):
    i = np.arange(P)
    ii = i[None, :].astype(np.float64)
    jj = i[:, None].astype(np.float64)
    c = np.zeros((14, P, P), np.float32)
    c[0] = np.eye(P)
    c[1] = (ii >= jj)
    c[2] = (jj > ii)
    c[3] = np.maximum(ii - jj, 0)
    c[4] = np.maximum(jj - ii, 0)
    c[5] = np.broadcast_to(ii + 1, (P, P))
    c[6] = np.broadcast_to(128 - ii, (P, P))
    c[7] = (jj <= ii)
    c[8] = (jj >= ii)
    c[9] = (jj > ii)
    c[10] = (jj < ii)
    c[11] = 1.0
    pm_ = _perm()
    c[13][pm_, np.arange(P)] = 1.0
    c[12, :, 0] = 127 - i
    c[12, :, 1] = i
    c[12, :, 2] = 128.0
    rows_ = NL * P // 64
    r_idx, c_idx = np.meshgrid(np.arange(rows_), np.arange(64), indexing="ij")
    r_idx = r_idx.reshape(-1).astype(np.float32)
    c_idx = c_idx.reshape(-1).astype(np.float32)
    n_freq = DK // 4
    inv_freq = (np.float32(10000.0) ** (-np.arange(n_freq, dtype=np.float32) / np.float32(n_freq))).astype(np.float32)
    ang_r = r_idx[:, None] * inv_freq
    ang_c = c_idx[:, None] * inv_freq
    ang = np.stack([ang_r, ang_r, ang_c, ang_c], axis=1).reshape(-1, DK)
    cos = np.cos(ang).astype(np.float32)
    sin = np.sin(ang).astype(np.float32)
    sign = np.ones(DK, np.float32)
    d = np.arange(DK)
    sign[(d // 32) % 2 == 0] = -1.0
    cosT = np.ascontiguousarray(cos.T)
    sinT = np.ascontiguousarray((sin * sign[None, :]).T)
    return c, cosT, sinT


_PERM = None


def _perm():
    d = np.arange(DK)
    a = d // 64
    pr = (d // 32) % 2
    f = d % 32
    return a * 64 + (1 - pr) * 32 + f


def _kc_layout(w):
    n = w.shape[1]
    return np.ascontiguousarray(w.reshape(KC, P, n).transpose(1, 0, 2))


def prepare_shared(inp, NL):
    L = inp["w_in"].shape[0]
    w_in = inp["w_in"]
    perm = _perm()
    cst, cosT, sinT = _consts(NL)
    sh = {"cst": cst, "cosT": cosT, "sinT": sinT}
    sh["wada"] = np.stack([_kc_layout(inp["w_ada"][l]) for l in range(L)])
    ba = inp["b_ada"]
    sh["badac"] = np.ascontiguousarray(ba[:, :2048].reshape(L, 16, P).transpose(0, 2, 1))
    sh["bgate"] = np.ascontiguousarray(np.broadcast_to(ba[:, None, 2048:], (L, P, D)))
    sh["ngain"] = np.ascontiguousarray(inp["norm_gain"].reshape(L, KC, P).transpose(0, 2, 1))
    sh["rgain"] = np.ascontiguousarray(np.broadcast_to(inp["ret_norm_gain"][:, None, :], (L, P, D)))
    sh["ggain"] = np.ascontiguousarray(np.broadcast_to(inp["gla_norm_gain"][:, None, :], (L, P, D)))
    sh["fgain"] = np.ascontiguousarray(np.broadcast_to(inp["final_norm_gain"][None, :], (P, D)))
    sh["rdec"] = np.ascontiguousarray(np.broadcast_to(inp["ret_decay"].reshape(L, 1, 8), (L, P, 8)))
    wup = np.zeros((L, 17, NH, 2, P), np.float32)
    for dr in range(2):
        wup[:, :16, :, dr, :] = inp["gla_w_up"][:, dr].reshape(L, 16, NH, P)
        wup[:, 16, :, dr, :] = inp["gla_b_up"][:, dr].reshape(L, NH, P)
    sh["wup"] = wup.reshape(L, 17, 1024)
    wret = np.zeros((L, NH, P, KC, 1024), np.float32)
    wgla = np.zeros((L, NH, P, KC, 768), np.float32)
    for l in range(L):
        for h in range(NH):
            q = w_in[l][:, h * 128:(h + 1) * 128]
            k = w_in[l][:, 512 + h * 128:512 + (h + 1) * 128]
            v = w_in[l][:, 1024 + h * 256:1024 + (h + 1) * 256]
            g = w_in[l][:, 2048 + h * 256:2048 + (h + 1) * 256]
            wret[l, h] = _kc_layout(np.concatenate([q, q[:, perm], k, k[:, perm], v, g], axis=1))
            q = w_in[l][:, 3072 + h * 128:3072 + (h + 1) * 128]
            k = w_in[l][:, 3584 + h * 128:3584 + (h + 1) * 128]
            v = w_in[l][:, 4096 + h * 256:4096 + (h + 1) * 256]
            g = w_in[l][:, 5120 + h * 256:5120 + (h + 1) * 256]
            wgla[l, h] = _kc_layout(np.concatenate([q, k, v, g], axis=1))
    sh["wret"] = wret
    sh["wgla"] = wgla
    sh["wglr"] = np.stack([_kc_layout(w_in[l][:, 6144:6160]) for l in range(L)])
    sh["wm"] = np.stack([np.stack([_kc_layout(w_in[l][:, 6160:7184]), _kc_layout(w_in[l][:, 7184:8208])]) for l in range(L)])
    sh["wbr"] = np.stack([np.stack([_kc_layout(inp["w_branch_ret"][l]), _kc_layout(inp["w_branch_gla"][l])]) for l in range(L)])
    sh["wout"] = np.stack([_kc_layout(inp["w_out"][l]) for l in range(L)])
    return sh


def run(inp, n_cores=None):
    inp = {k: np.asarray(v, dtype=np.float32) for k, v in inp.items()}
    B, SEQ, _ = inp["x"].shape
    CTX = inp["ctx"].shape[1]
    L = inp["w_in"].shape[0]
    NL = SEQ // P
    NCT = CTX // P
    nc = build_program(NCT, NL, L)
    sh = prepare_shared(inp, NL)
    in_maps = []
    for b in range(B):
        m = dict(sh)
        m["x"] = np.ascontiguousarray(inp["x"][b])
        m["ctx"] = np.ascontiguousarray(inp["ctx"][b])
        cc = np.stack([inp["c"][b].reshape(KC, P).T, inp["c_ctx"].reshape(KC, P).T], axis=2)
        m["ccol"] = np.ascontiguousarray(cc.reshape(P, 16))
        in_maps.append(m)
    res = run_bass_kernel_spmd(nc, in_maps, core_ids=list(range(B)))
    return np.stack([r["y"] for r in res.results], axis=0).astype(np.float32)


def kernel(**inputs):
    return run(inputs)
```
